# Optimizing a Trainium2 kernel written in Bass

```python
import jax, jax.numpy as jnp
from jax import lax
import numpy as np

D_MODEL = 2048
BATCH = 16
SEQ = 256
DEPTH = 2
DEC_BATCH = 2
DEC_SEQ = 4096
PAST_LEN = 512

GRID_W = 64
N_DIR = 2
MIX_WIDTH = D_MODEL
GLA_WIDTH = MIX_WIDTH // 2
GLA_HEADS = 4
GLA_DV = GLA_WIDTH // GLA_HEADS
GLA_DK = GLA_DV // 2
GLA_QK = GLA_HEADS * GLA_DK
GLA_RANK = 16
GLA_TAU = 16.0
HGRN_WIDTH = MIX_WIDTH - GLA_WIDTH
HGRN_EXPAND = 128
HGRN_HEADS = HGRN_WIDTH // HGRN_EXPAND
HGRN_DV = HGRN_WIDTH // HGRN_HEADS
HGRN_QK = HGRN_HEADS * HGRN_EXPAND
IN_WIDTH = 2 * GLA_QK + 2 * GLA_WIDTH + N_DIR * GLA_RANK + (1 + N_DIR) * HGRN_QK + 2 * HGRN_WIDTH
MLP_HIDDEN = 4 * D_MODEL
CHUNK = 32
EPS = 1e-6

kernel_name = 'hybrid_gla_hgrn2_diffusion_step'


def rmsnorm(x, g):
    xf = x.astype(jnp.float32)
    y = xf * lax.rsqrt(jnp.mean(xf * xf, axis=-1, keepdims=True) + EPS)
    return (y * g.astype(jnp.float32)).astype(x.dtype)


def chunk_gla(q, k, v, logd, s0):
    B, T, H, _ = q.shape
    V = v.shape[-1]
    n = T // CHUNK

    def blocks(a):
        return a.astype(jnp.float32).reshape(B, n, CHUNK, H, a.shape[-1]).transpose(1, 0, 3, 2, 4)

    causal = jnp.tril(jnp.ones((CHUNK, CHUNK), dtype=bool))

    def step(S, blk):
        qc, kc, vc, gc = blk
        b = jnp.cumsum(gc, axis=2)
        o_inter = jnp.einsum('bhtk,bhkv->bhtv', qc * jnp.exp(b), S)
        diff = jnp.where(causal[None, None, :, :, None],
                         b[:, :, :, None, :] - b[:, :, None, :, :], -jnp.inf)
        att = jnp.einsum('bhtk,bhsk,bhtsk->bhts', qc, kc, jnp.exp(diff))
        o = o_inter + jnp.einsum('bhts,bhsv->bhtv', att, vc)
        b_last = b[:, :, -1, :]
        S = jnp.exp(b_last)[..., None] * S + jnp.einsum(
            'bhsk,bhsv->bhkv', kc * jnp.exp(b_last[:, :, None, :] - b), vc)
        return S, o

    S, o = lax.scan(step, s0.astype(jnp.float32), (blocks(q), blocks(k), blocks(v), blocks(logd)))
    o = o.transpose(1, 0, 3, 2, 4).reshape(B, T, H, V)
    return o, S


def bidir_scan(q, k_f, k_b, v, logd_f, logd_b, s0_f, s0_b):
    o_f, s_f = chunk_gla(q, k_f, v, logd_f, s0_f)
    flip = lambda a: jnp.flip(a, axis=1)
    o_b, s_b = chunk_gla(flip(q), flip(k_b), flip(v), flip(logd_b), s0_b)
    return o_f + flip(o_b), s_f, s_b


def hybrid_mixer(h, w_in, gla_gate_w, gla_gate_b, gla_norm_g, lb, hgrn_norm_g, s_gla, s_hgrn):
    B, T, _ = h.shape
    sizes = [GLA_QK, GLA_QK, GLA_WIDTH, GLA_WIDTH, N_DIR * GLA_RANK,
             HGRN_QK, N_DIR * HGRN_QK, HGRN_WIDTH, HGRN_WIDTH]
    cuts = np.cumsum(sizes)[:-1].tolist()
    z = (h @ w_in).astype(jnp.float32)
    gq, gk, gv, gr, glr, hq, hf, hi, hg = jnp.split(z, cuts, axis=-1)
    heads = lambda a, n: a.reshape(B, T, n, -1)

    glogit = jnp.einsum('btdr,drk->btdk', glr.reshape(B, T, N_DIR, GLA_RANK),
                        gla_gate_w.astype(jnp.float32)) + gla_gate_b.astype(jnp.float32)
    glogd = jax.nn.log_sigmoid(glogit) / GLA_TAU
    gkh = heads(gk, GLA_HEADS)
    o_gla, sg_f, sg_b = bidir_scan(
        heads(gq, GLA_HEADS) * GLA_DK ** -0.5, gkh, gkh, heads(gv, GLA_HEADS),
        heads(glogd[:, :, 0], GLA_HEADS), heads(glogd[:, :, 1], GLA_HEADS),
        s_gla[:, 0], s_gla[:, 1])
    o_gla = rmsnorm(o_gla, gla_norm_g).reshape(B, T, GLA_WIDTH) * jax.nn.silu(gr)

    hf = hf.reshape(B, T, N_DIR, HGRN_QK)
    f = lb + (1.0 - lb) * jax.nn.sigmoid(hf)
    hk = (1.0 - lb) * jax.nn.sigmoid(-hf)
    hlogf = jnp.log(f)
    o_h, sh_f, sh_b = bidir_scan(
        heads(jax.nn.silu(hq), HGRN_HEADS), heads(hk[:, :, 0], HGRN_HEADS), heads(hk[:, :, 1], HGRN_HEADS),
        heads(hi, HGRN_HEADS), heads(hlogf[:, :, 0], HGRN_HEADS), heads(hlogf[:, :, 1], HGRN_HEADS),
        s_hgrn[:, 0], s_hgrn[:, 1])
    o_h = rmsnorm(o_h, hgrn_norm_g).reshape(B, T, HGRN_WIDTH) * jax.nn.silu(hg)

    out = jnp.concatenate([o_gla, o_h], axis=-1).astype(h.dtype)
    return out, jnp.stack([sg_f, sg_b], axis=1), jnp.stack([sh_f, sh_b], axis=1)


def to_col_major(x):
    B, T, D = x.shape
    rows = T // GRID_W
    return x.reshape(B, rows, GRID_W, D).transpose(0, 2, 1, 3).reshape(B, T, D)


def from_col_major(x):
    B, T, D = x.shape
    rows = T // GRID_W
    return x.reshape(B, GRID_W, rows, D).transpose(0, 2, 1, 3).reshape(B, T, D)


def trunk_layer(x, mod, col_major, s_gla, s_hgrn, lb, norm1_g, norm2_g, w_in, gla_gate_w, gla_gate_b,
                gla_norm_g, hgrn_norm_g, w_out, w_mlp1, w_mlp2):
    sh1, sc1, gt1, sh2, sc2, gt2 = jnp.split(mod, 6, axis=-1)
    h = rmsnorm(x, norm1_g) * (1 + sc1) + sh1
    if col_major:
        h = to_col_major(h)
    m, sg, sh = hybrid_mixer(h, w_in, gla_gate_w, gla_gate_b, gla_norm_g, lb, hgrn_norm_g, s_gla, s_hgrn)
    if col_major:
        m = from_col_major(m)
    x = x + gt1 * (m @ w_out)
    h = rmsnorm(x, norm2_g) * (1 + sc2) + sh2
    u = jnp.square(jax.nn.relu(h @ w_mlp1))
    x = x + gt2 * (u @ w_mlp2)
    return x, sg, sh


def setup_inputs(seed: int = 0) -> dict:
    key = jax.random.key(seed)
    ks = jax.random.split(key, 24)
    nrm = lambda k, shape, s: jax.random.normal(k, shape, jnp.float32) * s
    D = D_MODEL
    return {
        'x_prompt': nrm(ks[0], (BATCH, SEQ, D), 1.0),
        'x_sample': nrm(ks[1], (DEC_BATCH, DEC_SEQ, D), 1.0),
        'state_gla': nrm(ks[2], (DEC_BATCH, DEPTH, N_DIR, GLA_HEADS, GLA_DK, GLA_DV), 1.0),
        'state_hgrn': nrm(ks[3], (DEC_BATCH, DEPTH, N_DIR, HGRN_HEADS, HGRN_EXPAND, HGRN_DV), 0.5),
        'c': nrm(ks[4], (DEC_BATCH, D), 1.0),
        'c_ctx': nrm(ks[5], (D,), 1.0),
        'ada_w': nrm(ks[6], (DEPTH, D, 6 * D), 0.5 * D ** -0.5),
        'ada_b': nrm(ks[7], (DEPTH, 6 * D), 0.01),
        'norm1_g': 1.0 + nrm(ks[8], (DEPTH, D), 0.02),
        'norm2_g': 1.0 + nrm(ks[9], (DEPTH, D), 0.02),
        'w_in': nrm(ks[10], (DEPTH, D, IN_WIDTH), D ** -0.5),
        'gla_gate_w': nrm(ks[11], (DEPTH, N_DIR, GLA_RANK, GLA_QK), GLA_RANK ** -0.5),
        'gla_gate_b': nrm(ks[12], (DEPTH, N_DIR, GLA_QK), 0.1),
        'gla_norm_g': 1.0 + nrm(ks[13], (DEPTH, GLA_DV), 0.02),
        'hgrn_lb': 1.0 + nrm(ks[14], (DEPTH, N_DIR, HGRN_QK), 0.1),
        'hgrn_norm_g': 1.0 + nrm(ks[15], (DEPTH, HGRN_DV), 0.02),
        'w_out': nrm(ks[16], (DEPTH, MIX_WIDTH, D), MIX_WIDTH ** -0.5),
        'w_mlp1': nrm(ks[17], (DEPTH, D, MLP_HIDDEN), D ** -0.5),
        'w_mlp2': nrm(ks[18], (DEPTH, MLP_HIDDEN, D), MLP_HIDDEN ** -0.5),
        'final_g': 1.0 + nrm(ks[19], (D,), 0.02),
    }


def reference(x_prompt, x_sample, state_gla, state_hgrn, c, c_ctx, ada_w, ada_b, norm1_g, norm2_g, w_in,
              gla_gate_w, gla_gate_b, gla_norm_g, hgrn_lb, hgrn_norm_g, w_out, w_mlp1, w_mlp2, final_g):
    p = jax.nn.softmax(hgrn_lb.astype(jnp.float32), axis=0)
    lb_all = jnp.cumsum(p, axis=0) - p[:1]

    Bp = x_prompt.shape[0]
    zero_gla = jnp.zeros((Bp, N_DIR, GLA_HEADS, GLA_DK, GLA_DV), jnp.float32)
    zero_hgrn = jnp.zeros((Bp, N_DIR, HGRN_HEADS, HGRN_EXPAND, HGRN_DV), jnp.float32)
    xp = x_prompt
    new_gla, new_hgrn = [], []
    for l in range(DEPTH):
        mod = (jax.nn.silu(c_ctx) @ ada_w[l] + ada_b[l])[None, None, :]
        xp, sg, sh = trunk_layer(xp, mod, False, zero_gla, zero_hgrn, lb_all[l], norm1_g[l], norm2_g[l],
                                 w_in[l], gla_gate_w[l], gla_gate_b[l], gla_norm_g[l], hgrn_norm_g[l],
                                 w_out[l], w_mlp1[l], w_mlp2[l])
        new_gla.append(sg)
        new_hgrn.append(sh)
    y_prompt = rmsnorm(xp, final_g)

    xs = x_sample
    for l in range(DEPTH):
        mod = (jax.nn.silu(c) @ ada_w[l] + ada_b[l])[:, None, :]
        xs, _, _ = trunk_layer(xs, mod, l % 2 == 1, state_gla[:, l], state_hgrn[:, l], lb_all[l],
                               norm1_g[l], norm2_g[l], w_in[l], gla_gate_w[l], gla_gate_b[l],
                               gla_norm_g[l], hgrn_norm_g[l], w_out[l], w_mlp1[l], w_mlp2[l])
    y_sample = rmsnorm(xs, final_g)

    new_state_gla = jnp.stack(new_gla, axis=1).astype(x_prompt.dtype)
    new_state_hgrn = jnp.stack(new_hgrn, axis=1).astype(x_prompt.dtype)
    return (y_prompt, y_sample, new_state_gla, new_state_hgrn)
```

```python
import contextlib
import numpy as np
import concourse.bass as bass
import concourse.mybir as mybir
from concourse.bass_utils import run_bass_kernel_spmd

F32 = mybir.dt.float32
BF16 = mybir.dt.bfloat16
AF = mybir.ActivationFunctionType
ALU = mybir.AluOpType

ENGS = ["tensor", "vector", "scalar", "gpsimd", "sync"]
NDSEM = 8

L = 2
D = 2048
KC = 16
NPT = 512
NST = 1024
TS = 4096
WG = 2208
C_Q = (0, 128, 256)
C_GATE = 384
C_GLR = (896, 912)
C_V = 928
C_K = (1440, 1824)
EPS = 1e-6
NCON = 16
PIPE = True
PIPE_S_DIRS = (0, 1)


class Buf:
    __slots__ = ("lw", "rd")

    def __init__(self):
        self.lw = None
        self.rd = []


class Prog:
    def __init__(self):
        self.q = {e: [] for e in ENGS}
        self.cnt = {}
        self.seen = {e: {} for e in ENGS}
        self.dma_rr = {e: 0 for e in ENGS}
        self.open = {e: False for e in ENGS}

    @staticmethod
    def _deps(reads, writes):
        deps = []
        for b in reads:
            if b.lw is not None:
                deps.append(b.lw)
        for b in writes:
            if b.lw is not None:
                deps.append(b.lw)
            deps.extend(b.rd)
        return deps

    def _waits(self, eng, deps):
        best = {}
        for k, v in deps:
            if v > best.get(k, 0):
                best[k] = v
        out = []
        for k, v in best.items():
            if eng == "tensor" and k == "e_tensor":
                continue
            if self.seen[eng].get(k, 0) < v:
                self.seen[eng][k] = v
                out.append((k, v))
        return out

    @staticmethod
    def _mark(reads, writes, key, v):
        for b in reads:
            b.rd.append((key, v))
        for b in writes:
            b.lw = (key, v)
            b.rd = []

    def op(self, eng, fn, reads=(), writes=(), signal=True):
        waits = self._waits(eng, self._deps(reads, writes))
        key = "e_" + eng
        v = self.cnt.get(key, 0) + 1
        if signal:
            self.cnt[key] = v
            self.q[eng].append((waits, fn, key, 1))
            self.open[eng] = False
        else:
            self.q[eng].append((waits, fn, None, 0))
            self.open[eng] = True
        self._mark(reads, writes, key, v)

    def dma(self, eng, out, in_, reads=(), writes=(), **kw):
        slot = self.dma_rr[eng] % NDSEM
        self.dma_rr[eng] += 1
        key = "d_%s_%d" % (eng, slot)
        prev = self.cnt.get(key, 0)
        deps = self._deps(reads, writes)
        if prev:
            deps.append((key, prev))
        waits = self._waits(eng, deps)
        v = prev + 16
        self.cnt[key] = v
        self.q[eng].append((waits, lambda e: e.dma_start(out=out, in_=in_, **kw), key, 16))
        self._mark(reads, writes, key, v)

    def barrier(self):
        for e in ENGS:
            assert not self.open[e], "open unsignalled group on %s" % e
        tot = list(self.cnt.items())
        for e in ENGS:
            waits = self._waits(e, tot)
            if waits:
                self.q[e].append((waits, None, None, 0))

    def emit(self, nc, stack):
        self.barrier()
        sems = {k: stack.enter_context(nc.semaphore(k)) for k in self.cnt}
        block = stack.enter_context(nc.Block())
        prog = self

        def run(engname, e):
            for waits, fn, key, amt in prog.q[engname]:
                for k, v in waits:
                    e.wait_ge(sems[k], v)
                if fn is None:
                    continue
                ins = fn(e)
                if key is not None:
                    ins.then_inc(sems[key], amt)

        @block.tensor
        def _(e):
            run("tensor", e)

        @block.vector
        def _(e):
            run("vector", e)

        @block.scalar
        def _(e):
            run("scalar", e)

        @block.gpsimd
        def _(e):
            run("gpsimd", e)

        @block.sync
        def _(e):
            run("sync", e)


class Arena:
    def __init__(self, t, nwords):
        self.t = t
        self.n = nwords
        self.top = 0
        self.stack = []

    def alloc(self, shape, dtype, parts=128):
        nel = int(np.prod(shape))
        nw = (nel * (2 if dtype == BF16 else 4) + 3) // 4
        nw = (nw + 7) // 8 * 8
        off = self.top
        self.top += nw
        assert self.top <= self.n, "arena overflow %d > %d" % (self.top, self.n)
        v = self.t[0:parts, off:off + nw]
        if dtype == BF16:
            v = v.bitcast(BF16)
        v = v[:, 0:nel]
        if len(shape) == 2:
            v = v.rearrange("p (a b) -> p a b", a=shape[0])
        elif len(shape) == 3:
            v = v.rearrange("p (a b c) -> p a b c", a=shape[0], b=shape[1])
        return v

    def mark(self):
        self.stack.append(self.top)

    def release(self):
        self.top = self.stack.pop()


class _Stop(Exception):
    pass


def build_program(stop=None, fake_cc=False, ncores=8, lite=()):
    nc = bass.Bass("TRN2", target_bir_lowering=False)
    P = Prog()

    def chk(name):
        if stop == name:
            raise _Stop()

    def din(name, shape, dt=F32):
        if name in lite:
            shape = [1] * len(shape)
        return nc.dram_tensor(name, list(shape), dt, kind="ExternalInput").ap()

    def dout(name, shape):
        return nc.dram_tensor(name, list(shape), F32, kind="ExternalOutput").ap()

    def dint(name, shape, dt):
        return nc.dram_tensor(name, list(shape), dt, kind="Internal").ap()

    xp_d = din("xp", [NPT, D])
    xs_d = din("xs", [NST, D])
    sgla_d = din("sgla", [L, 2, 128, 256])
    shg_d = din("shg", [L, 2, 2, 128, 128])
    cT_d = din("cT", [128, KC, 2])
    adaw_d = din("adaw", [D, 6144])
    adab_d = din("adab", [128, 48])
    g1_d = din("g1T", [L, 128, KC])
    g2_d = din("g2T", [L, 128, KC])
    fg_d = din("fgT", [128, KC])
    wing_d = din("wing", [L, 4, D, WG])
    wins_d = din("wins", [L, D, WG])
    gw_d = din("gw", [L, 4, 32, 2, 128])
    gws_d = din("gws", [L, 32, 2, 128])
    lbg_d = din("lbg", [4, L, 2, 256])
    lbs_d = din("lbs", [L, 2, 256])
    gn_d = din("gn", [L, 128, 3])
    wout_d = din("wout", [L, D, D])
    w1_d = din("w1", [L, D, 4 * D])
    w2_d = din("w2", [L, 4 * D, D])
    con_d = din("consts", [128, NCON, 128])
    sel_d = din("sel", [128, 4])

    yp_d = dout("yp", [NPT, D])
    ys_d = dout("ys", [NST, D])
    ng_d = dout("ng", [2, L, 2, 4, 128, 256])
    nh_d = dout("nh", [2, L, 2, 8, 128, 128])

    xT_d = dint("xT_scr", [KC, 128, NPT + NST], F32)
    hb_d = dint("h_bounce", [NST, D], BF16)
    hf_d = dint("h_full", [4, 4, 256, D], BF16)
    mb_d = dint("m_bounce", [TS, 512], BF16)
    mf_d = dint("m_full", [4, 4, 1024, 512], BF16)
    modb_d = dint("mod_bounce", [128, 96], F32)
    modf_d = dint("mod_full", [512, 96], F32)
    of_d = dint("of_scr", [32, 128, 4, 128], F32)
    RG = [[0, 1, 2, 3], [4, 5, 6, 7]] if ncores == 8 else [[0, 1, 2, 3]]

    B_xT = [Buf() for _ in range(3)]
    B_hb = Buf()
    B_hf = Buf()
    B_mb = [Buf() for _ in range(32)]
    B_mf = Buf()
    B_of = [Buf() for _ in range(32)]
    B_out = Buf()

    with contextlib.ExitStack() as st:
        AW = 52000
        arena_t = st.enter_context(nc.sbuf_tensor("arena", [128, AW], F32))
        A = Arena(arena_t, AW)
        pbank = [st.enter_context(nc.psum_tensor("pb%d" % i, [128, 512], F32)) for i in range(8)]
        Bp = [Buf() for _ in range(8)]
        ptb = pbank[7][:, :].bitcast(BF16)

        def body():
            def act(out, in_, func, R, W, **kw):
                P.op("scalar", lambda e: e.activation(out=out, in_=in_, func=func, **kw), R, W)

            def vtt(out, a, b, op, R, W):
                P.op("vector", lambda e: e.tensor_tensor(out=out, in0=a, in1=b, op=op), R, W)

            def gtt(out, a, b, op, R, W):
                P.op("gpsimd", lambda e: e.tensor_tensor(out=out, in0=a, in1=b, op=op), R, W)

            def vts(out, a, s1, s2, op0, op1, R, W):
                P.op("vector", lambda e: e.tensor_scalar(out=out, in0=a, scalar1=s1, scalar2=s2, op0=op0, op1=op1), R, W)

            def gts(out, a, s1, s2, op0, op1, R, W):
                P.op("gpsimd", lambda e: e.tensor_scalar(out=out, in0=a, scalar1=s1, scalar2=s2, op0=op0, op1=op1), R, W)

            def vstt(out, a, s, b, op0, op1, R, W):
                P.op("vector", lambda e: e.scalar_tensor_tensor(out=out, in0=a, scalar=s, in1=b, op0=op0, op1=op1), R, W)

            def vcopy(out, in_, R, W):
                P.op("vector", lambda e: e.tensor_copy(out=out, in_=in_), R, W)

            def gcopy(out, in_, R, W):
                P.op("gpsimd", lambda e: e.tensor_copy(out=out, in_=in_), R, W)

            def vrecip(out, in_, R, W):
                P.op("vector", lambda e: e.reciprocal(out=out, in_=in_), R, W)

            def mm(out, lhsT, rhs, start, stop, R, W, signal=True):
                P.op("tensor", lambda e: e.matmul(out=out, lhsT=lhsT, rhs=rhs, start=start, stop=stop), R, W, signal)

            def tr(out, in_, ident, R, W, signal=True):
                P.op("tensor", lambda e: e.transpose(out=out, in_=in_, identity=ident), R, W, signal)

            def memset(eng, ap, val, W):
                P.op(eng, lambda e: e.memset(ap, val), (), W)

            def allgather(src, dst, R, W):
                if fake_cc:
                    P.dma("sync", dst[0:src.shape[0]], src, reads=R, writes=W)
                else:
                    P.op("gpsimd", lambda e: e.collective_compute("AllGather", ALU.bypass, replica_groups=RG,
                                                                  ins=[src], outs=[dst]), R, W)

            con = A.alloc([NCON, 128], F32)
            B_con = Buf()
            identf = con[:, 0, :]
            TRI = {
                (False, 0): (con[:, 1, :], con[:, 3, :]), (False, 1): (con[:, 2, :], con[:, 4, :]),
                (True, 0): (con[:, 5, :], con[:, 7, :]), (True, 1): (con[:, 6, :], con[:, 8, :]),
            }
            MASK3 = (con[:, 10:13, :].rearrange("p a b -> p (a b)"), con[:, 13:16, :].rearrange("p a b -> p (a b)"))
            cm = con[:, 9, 0:4]
            identb = A.alloc([128], BF16)
            B_idb = Buf()
            onesb = A.alloc([128], BF16)
            B_ones = Buf()
            mv = A.alloc([L * 2 * 6, KC], F32)
            B_mv = Buf()
            fgT = A.alloc([KC], F32)
            B_fg = Buf()
            gnT = A.alloc([L, 3], F32)
            B_gn = Buf()

            def MV(l, s, w):
                return mv[:, (l * 2 + s) * 6 + w, :]

            P.dma("sync", con, con_d, writes=[B_con])
            P.dma("gpsimd", identb, con_d[:, 0, :], writes=[B_idb])
            memset("vector", onesb, 1.0, [B_ones])
            P.dma("sync", fgT, fg_d, writes=[B_fg])
            P.dma("sync", gnT, gn_d.rearrange("l p c -> p l c"), writes=[B_gn])

            A.mark()
            cT = A.alloc([KC, 2], F32)
            B_cT = Buf()
            sc_e = A.alloc([KC, 2], F32)
            scT = A.alloc([KC, 2], F32)
            B_sc = Buf()
            adb = A.alloc([48], F32)
            B_adb = Buf()
            g12 = A.alloc([2 * L, KC], F32)
            B_g12 = Buf()
            awt = [A.alloc([KC, 512], F32) for _ in range(2)]
            B_aw = [Buf() for _ in range(2)]
            modown = A.alloc([48, 2], F32)
            B_mo = Buf()
            modall = A.alloc([4, 96], F32)
            B_ma = Buf()
            P.dma("sync", cT, cT_d, writes=[B_cT])
            P.dma("sync", adb, adab_d, writes=[B_adb])
            for l in range(L):
                P.dma("sync", g12[:, l, :], g1_d[l], writes=[B_g12])
                P.dma("sync", g12[:, L + l, :], g2_d[l], writes=[B_g12])
            act(sc_e, cT, AF.Exp, [B_cT], [B_sc], scale=-1.0)
            vts(sc_e, sc_e, 1.0, None, ALU.add, ALU.bypass, [], [B_sc])
            vrecip(sc_e, sc_e, [], [B_sc])
            vtt(scT, sc_e, cT, ALU.mult, [B_cT], [B_sc])
            adaw_v = adaw_d.rearrange("(k p) n -> p k n", p=128)
            for t in range(12):
                s = t % 2
                P.dma("sync", awt[s], adaw_v[:, :, t * 512:(t + 1) * 512], writes=[B_aw[s]])
                for j in range(4):
                    n = t * 4 + j
                    for kc in range(KC):
                        mm(pbank[0][:, 2 * n:2 * n + 2], awt[s][:, kc, j * 128:(j + 1) * 128], scT[:, kc, :],
                           kc == 0, kc == KC - 1, [B_aw[s], B_sc], [Bp[0]], signal=(kc == KC - 1))
            pm = pbank[0][:, 0:96].rearrange("p (n v) -> p n v", v=2)
            for v in range(2):
                vtt(modown[:, :, v], pm[:, :, v], adb, ALU.add, [B_adb], [B_mo, Bp[0]])
            P.dma("sync", modb_d, modown.rearrange("p n v -> p (n v)"), reads=[B_mo], writes=[B_hb])
            B_modf = Buf()
            allgather(modb_d, modf_d, [B_hb], [B_modf])
            P.dma("sync", modall, modf_d.rearrange("(r p) f -> p r f", p=128), reads=[B_modf], writes=[B_ma])
            modn = modall.rearrange("p r (j v) -> p (r j) v", v=2)
            for l in range(L):
                for s in range(2):
                    def sl(w):
                        return modn[:, l * 96 + w * 16:l * 96 + (w + 1) * 16, s]
                    vstt(MV(l, s, 0), sl(1), 1.0, g12[:, l, :], ALU.add, ALU.mult, [B_ma, B_g12], [B_mv])
                    vcopy(MV(l, s, 1), sl(0), [B_ma], [B_mv])
                    vcopy(MV(l, s, 2), sl(2), [B_ma], [B_mv])
                    vstt(MV(l, s, 3), sl(4), 1.0, g12[:, L + l, :], ALU.add, ALU.mult, [B_ma, B_g12], [B_mv])
                    vcopy(MV(l, s, 4), sl(3), [B_ma], [B_mv])
                    vcopy(MV(l, s, 5), sl(5), [B_ma], [B_mv])
            P.barrier()
            A.release()
            chk("p0")

            A.mark()
            xtok = [A.alloc([D], F32) for _ in range(2)]
            B_xtok = [Buf() for _ in range(2)]
            xTb = [A.alloc([KC, 512], F32) for _ in range(2)]
            B_xTb = [Buf() for _ in range(2)]
            xTv = xT_d.rearrange("k p t -> p k t")
            for blk in range(3):
                src = xp_d if blk == 0 else xs_d
                bs = blk % 2
                for t in range(4):
                    ts_ = t % 2
                    row0 = (t if blk == 0 else (blk - 1) * 4 + t) * 128
                    P.dma("sync", xtok[ts_], src[row0:row0 + 128, :], writes=[B_xtok[ts_]])
                    for q in range(4):
                        pb = q % 2
                        for i in range(4):
                            kc = q * 4 + i
                            tr(pbank[pb][:, i * 128:(i + 1) * 128], xtok[ts_][:, kc * 128:(kc + 1) * 128], identf,
                               [B_xtok[ts_], B_con], [Bp[pb]], signal=(i == 3))
                        o_ = xTb[bs][:, q * 4:(q + 1) * 4, t * 128:(t + 1) * 128]
                        i_ = pbank[pb][:, :].rearrange("p (a b) -> p a b", a=4)
                        if q % 2:
                            act(o_, i_, AF.Copy, [], [B_xTb[bs], Bp[pb]])
                        else:
                            vcopy(o_, i_, [], [B_xTb[bs], Bp[pb]])
                P.dma("sync", xTv[:, :, blk * 512:(blk + 1) * 512], xTb[bs], reads=[B_xTb[bs]], writes=[B_xT[blk]])
            P.barrier()
            A.release()
            chk("p0x")

            def rms_rstd(srcs, n, ncols, B_src, sqt, B_sq, pb, rstd, B_rstd, inv_n):
                for k in range(n):
                    s = k % 2
                    act(sqt[s], srcs[k], AF.Square, B_src, [B_sq[s]])
                    mm(pbank[pb][:, 0:ncols], onesb, sqt[s], k == 0, k == n - 1, [B_ones, B_sq[s]], [Bp[pb]])
                vts(rstd, pbank[pb][:, 0:ncols], inv_n, EPS, ALU.mult, ALU.add, [], [B_rstd, Bp[pb]])
                act(rstd, rstd, AF.Ln, [], [B_rstd])
                act(rstd, rstd, AF.Exp, [], [B_rstd], scale=-0.5)

            PR = {}

            def p1(job, l):
                A.mark()
                mset = 0 if job == "p" else 1
                blks = [0] if job == "p" else [1, 2]
                xt = A.alloc([KC, 512], F32)
                B_xt = Buf()
                sqt = [A.alloc([512], BF16) for _ in range(2)]
                B_sq = [Buf(), Buf()]
                rstd = A.alloc([512], F32)
                B_rstd = Buf()
                tmp = [A.alloc([512], F32) for _ in range(2)]
                B_tmp = [Buf(), Buf()]
                if job == "p":
                    hT, B_hT = PR["hTp"], PR["B_hTp"]
                else:
                    hT = A.alloc([KC, 512], BF16)
                    B_hT = Buf()
                    htok = [A.alloc([D], BF16) for _ in range(2)]
                    B_htok = [Buf(), Buf()]
                for blk in blks:
                    P.dma("sync", xt, xTv[:, :, blk * 512:(blk + 1) * 512], reads=[B_xT[blk]], writes=[B_xt])
                    rms_rstd([xt[:, kc, :] for kc in range(KC)], KC, 512, [B_xt], sqt, B_sq, 0, rstd, B_rstd, 1.0 / D)
                    for kc in range(KC):
                        s = kc % 2
                        vtt(tmp[s], xt[:, kc, :], rstd, ALU.mult, [B_xt, B_rstd], [B_tmp[s]])
                        act(hT[:, kc, :], tmp[s], AF.Identity, [B_tmp[s], B_mv], [B_hT],
                            scale=MV(l, mset, 0)[:, kc:kc + 1], bias=MV(l, mset, 1)[:, kc:kc + 1])
                    if job == "s":
                        for t in range(4):
                            s = t % 2
                            for half in range(2):
                                for i in range(8):
                                    kc = half * 8 + i
                                    tr(ptb[:, i * 128:(i + 1) * 128], hT[:, kc, t * 128:(t + 1) * 128], identb,
                                       [B_hT, B_idb], [Bp[7]], signal=(i == 7))
                                vcopy(htok[s][:, half * 1024:(half + 1) * 1024], ptb, [], [B_htok[s], Bp[7]])
                            tt = (blk - 1) * 4 + t
                            if l % 2 == 0:
                                P.dma("sync", hb_d[tt * 128:(tt + 1) * 128, :], htok[s], reads=[B_htok[s]], writes=[B_hb])
                            else:
                                hbv = hb_d.rearrange("(c rl) d -> rl c d", rl=16)
                                for half in range(2):
                                    P.dma("sync", hbv[2 * tt + half], htok[s][half * 64:(half + 1) * 64, :],
                                          reads=[B_htok[s]], writes=[B_hb])
                if job == "s":
                    for j in range(4):
                        allgather(hb_d[j * 256:(j + 1) * 256, :], hf_d[j].rearrange("r i d -> (r i) d"), [B_hb], [B_hf])
                P.barrier()
                A.release()

            TC = [0]

            def mixer(job, l):
                A.mark()
                npass = 4 if job == "p" else 1
                col_major = (job == "s") and (l % 2 == 1)
                wg = A.alloc([KC, WG], BF16)
                B_wg = Buf()
                gwt = A.alloc([2, 128], BF16, parts=32)
                B_gwt = Buf()
                lbraw = A.alloc([L * 2 * 256], F32)
                B_lbraw = Buf()
                lbv = A.alloc([2, 256], F32)
                oml = A.alloc([2, 256], F32)
                B_lb = Buf()

                def T2(shape, dt):
                    return [A.alloc(shape, dt) for _ in range(2)], [Buf(), Buf()]

                gaug = [A.alloc([128], BF16, parts=32) for _ in range(2)]
                B_gaug = [Buf(), Buf()]
                qs, B_qs = T2([384], F32)
                k32, B_k32 = T2([384], F32)
                hfr, B_hfr = T2([256], F32)
                e0g, B_e0g = T2([128], F32)
                v_bf, B_vbf = T2([512], BF16)
                gsil, B_gsil = T2([512], F32)
                eg = A.alloc([512], F32)
                B_eg = Buf()
                fh = A.alloc([256], F32)
                B_fh = Buf()
                gg, B_gg = T2([384], F32)
                eq = A.alloc([256], F32)
                B_eq = Buf()
                k_bf = A.alloc([384], BF16)
                B_kbf = Buf()
                ET = A.alloc([384], F32)
                B_ET = Buf()
                EinvT = A.alloc([384], F32)
                B_EinvT = Buf()
                er = A.alloc([384], F32)
                B_er = Buf()
                qeT = A.alloc([384], BF16)
                B_qeT = Buf()
                keT = A.alloc([384], BF16)
                B_keT = Buf()
                kd = A.alloc([384], F32)
                B_kd = Buf()
                kdc = [A.alloc([384], BF16) for _ in range(4)]
                B_kdc = [Buf() for _ in range(4)]
                attm = A.alloc([384], BF16)
                B_attm = Buf()
                S32 = A.alloc([512], F32)
                B_S32 = Buf()
                Sbf = [A.alloc([512], BF16) for _ in range(4)]
                B_Sbf = [Buf() for _ in range(4)]
                otot = A.alloc([512], F32)
                B_otot = Buf()
                ofs, B_ofs = T2([512], F32)
                oTs, B_oTs = T2([512], F32)
                sq4 = A.alloc([512], BF16)
                B_sq4 = Buf()
                rstd4 = A.alloc([512], F32)
                B_rstd4 = Buf()
                mtmp = A.alloc([512], F32)
                B_mtmp = Buf()
                if job == "s":
                    htok, B_htok = T2([D], BF16)
                    hTs, B_hTs = T2([KC, 128], BF16)
                    mTs = A.alloc([4, 128], BF16)
                    B_mTs = Buf()
                    mtok, B_mtok = T2([512], BF16)
                for g_ in gaug:
                    memset("gpsimd", g_, 1.0, [B_gaug[0], B_gaug[1]])
                mbv = mb_d.rearrange("(row c) f -> c row f", c=64)
                PA, PB, PC, PD, PE_, PO, PU = 0, 1, 2, 3, 4, 5, 6

                def load_htok(ti, s):
                    if not col_major:
                        P.dma("sync", htok[s], hf_d[(ti % 8) // 2, ti // 8, ((ti % 8) % 2) * 128:((ti % 8) % 2) * 128 + 128, :],
                              reads=[B_hf], writes=[B_htok[s]])
                    else:
                        for half in range(2):
                            c = 2 * ti + half
                            for r in range(4):
                                P.dma("sync", htok[s][half * 64 + r * 16:half * 64 + (r + 1) * 16, :],
                                      hf_d[c // 16, r, (c % 16) * 16:(c % 16) * 16 + 16, :], reads=[B_hf], writes=[B_htok[s]])

                for ps_ in range(npass):
                    gp = ps_
                    wsrc = (wing_d[l, gp] if job == "p" else wins_d[l]).rearrange("(k p) n -> p k n", p=128)
                    for hh in range(2):
                        P.dma("gpsimd", wg[:, :, hh * 1104:(hh + 1) * 1104], wsrc[:, :, hh * 1104:(hh + 1) * 1104],
                              writes=[B_wg])
                    P.dma("gpsimd", gwt, (gw_d[l, gp] if job == "p" else gws_d[l]), writes=[B_gwt])
                    lsrc = (lbg_d[gp] if job == "p" else lbs_d).rearrange("l d n -> (l d n)").partition_broadcast(128)
                    P.dma("sync", lbraw, lsrc, writes=[B_lbraw])
                    lbv2 = lbv.rearrange("p d n -> p (d n)")
                    oml2 = oml.rearrange("p d n -> p (d n)")
                    if l == 0:
                        memset("vector", lbv2, 0.0, [B_lb])
                        memset("vector", oml2, 1.0, [B_lb])
                    else:
                        vtt(lbv2, lbraw[:, 512:1024], lbraw[:, 0:512], ALU.subtract, [B_lbraw], [B_lb])
                        act(lbv2, lbv2, AF.Exp, [], [B_lb], scale=-1.0)
                        vts(lbv2, lbv2, 1.0, None, ALU.add, ALU.bypass, [], [B_lb])
                        vrecip(lbv2, lbv2, [], [B_lb])
                        vts(oml2, lbv2, -1.0, 1.0, ALU.mult, ALU.add, [], [B_lb])
                    chk("mx_a")
                    nseq, nt = (2, 2) if job == "p" else (1, 32)

                    tiles = []
                    for d in range(2):
                        for seq in range(nseq):
                            order = list(range(nt)) if d == 0 else list(range(nt - 1, -1, -1))
                            for oi, ti in enumerate(order):
                                tiles.append((d, seq, oi, ti, oi == 0, oi == nt - 1))

                    def get_hT(idx):
                        d, seq, oi, ti, first, lastt = tiles[idx]
                        par = idx % 2
                        if job == "p":
                            c0 = seq * 256 + ti * 128
                            return [PR["hTp"][:, kc, c0:c0 + 128] for kc in range(KC)], PR["B_hTp"]
                        for half in range(2):
                            for i in range(8):
                                kc = half * 8 + i
                                tr(ptb[:, i * 128:(i + 1) * 128], htok[par][:, kc * 128:(kc + 1) * 128], identb,
                                   [B_htok[par], B_idb], [Bp[7]], signal=(i == 7))
                            vcopy(hTs[par][:, half * 8:(half + 1) * 8, :],
                                  ptb.rearrange("p (a b) -> p a b", a=8), [], [B_hTs[par], Bp[7]])
                        return [hTs[par][:, kc, :] for kc in range(KC)], B_hTs[par]

                    def proj_parts(idx):
                        d, seq, oi, ti, first, lastt = tiles[idx]
                        par = idx % 2
                        sl = []
                        if job == "p":
                            c0 = seq * 256 + ti * 128
                            hTk = [PR["hTp"][:, kc, c0:c0 + 128] for kc in range(KC)]
                            B_hTk = PR["B_hTp"]
                        else:
                            hTk = [hTs[par][:, kc, :] for kc in range(KC)]
                            B_hTk = B_hTs[par]

                            def p_h(half):
                                def f():
                                    if half == 0 and idx + 1 < len(tiles):
                                        load_htok(tiles[idx + 1][3], (idx + 1) % 2)
                                    for i in range(8):
                                        kc = half * 8 + i
                                        tr(ptb[:, i * 128:(i + 1) * 128], htok[par][:, kc * 128:(kc + 1) * 128], identb,
                                           [B_htok[par], B_idb], [Bp[7]], signal=(i == 7))
                                    vcopy(hTs[par][:, half * 8:(half + 1) * 8, :],
                                          ptb.rearrange("p (a b) -> p a b", a=8), [], [B_hTs[par], Bp[7]])
                                return f
                            sl.append(p_h(0))
                            sl.append(p_h(1))

                        def grp(out, wcol0, wn, fm):
                            def f():
                                for kc in range(KC):
                                    if fm:
                                        mm(out, wg[:, kc, wcol0:wcol0 + wn], hTk[kc], kc == 0, kc == KC - 1,
                                           [B_wg, B_hTk], [Bp[out_bank[id(out)]]], signal=(kc == KC - 1))
                                    else:
                                        mm(out, hTk[kc], wg[:, kc, wcol0:wcol0 + wn], kc == 0, kc == KC - 1,
                                           [B_wg, B_hTk], [Bp[out_bank[id(out)]]], signal=(kc == KC - 1))
                            return f
                        out_bank = {}

                        def reg(ap, bank):
                            out_bank[id(ap)] = bank
                            return ap
                        sl.append(grp(reg(pbank[PA][0:16, 384:512], PA), C_GLR[d], 16, True))
                        sl.append(grp(reg(pbank[PD][:, 0:384], PD), C_K[d], 384, False))

                        def p_glogit():
                            act(gaug[par][0:16, :], pbank[PA][0:16, 384:512], AF.Copy, [], [B_gaug[par], Bp[PA]])
                            mm(pbank[PD][:, 384:512], gaug[par], gwt[:, d, :], True, True, [B_gaug[par], B_gwt], [Bp[PD]])
                        sl.append(p_glogit)
                        sl.append(grp(reg(pbank[PC][:, 0:512], PC), C_V, 512, False))
                        for qi in range(3):
                            sl.append(grp(reg(pbank[PA][:, qi * 128:(qi + 1) * 128], PA), C_Q[qi], 128, True))
                        if d == 1:
                            for gi in range(4):
                                sl.append(grp(reg(pbank[PB][:, gi * 128:(gi + 1) * 128], PB), C_GATE + gi * 128, 128, True))

                        def tail():
                            vcopy(hfr[par], pbank[PD][:, 128:384], [], [B_hfr[par], Bp[PD]])
                            vcopy(k32[par][:, 0:128], pbank[PD][:, 0:128], [], [B_k32[par], Bp[PD]])
                            act(e0g[par], pbank[PD][:, 384:512], AF.Exp, [], [B_e0g[par], Bp[PD]], scale=-1.0)
                            act(fh, hfr[par], AF.Exp, [B_hfr[par]], [B_fh], scale=-1.0)
                            act(fh, fh, AF.Identity, [], [B_fh], bias=1.0)
                            vrecip(fh, fh, [], [B_fh])
                            gtt(fh, fh, oml[:, d, :], ALU.mult, [B_lb], [B_fh])
                            gtt(fh, fh, lbv[:, d, :], ALU.add, [B_lb], [B_fh])
                            act(gg[par][:, 128:384], fh, AF.Ln, [B_fh], [B_gg[par]])
                            act(k32[par][:, 128:384], fh, AF.Identity, [B_fh], [B_k32[par]], scale=-1.0, bias=1.0)
                            act(gg[par][:, 0:128], e0g[par], AF.Ln, [B_e0g[par]], [B_gg[par]], bias=1.0)
                            act(v_bf[par], pbank[PC][:, 0:512], AF.Copy, [], [B_vbf[par], Bp[PC]])
                            act(qs[par][:, 0:128], pbank[PA][:, 0:128], AF.Copy, [], [B_qs[par], Bp[PA]], scale=128.0 ** -0.5)
                            vcopy(qs[par][:, 128:384], pbank[PA][:, 128:384], [], [B_qs[par], Bp[PA]])
                            act(eq, qs[par][:, 128:384], AF.Exp, [B_qs[par]], [B_eq], scale=-1.0)
                            act(eq, eq, AF.Identity, [], [B_eq], bias=1.0)
                            vrecip(eq, eq, [], [B_eq])
                            vtt(qs[par][:, 128:384], qs[par][:, 128:384], eq, ALU.mult, [B_eq], [B_qs[par]])
                            if d == 1:
                                act(eg, pbank[PB][:, 0:512], AF.Exp, [], [B_eg, Bp[PB]], scale=-1.0)
                                act(eg, eg, AF.Identity, [], [B_eg], bias=1.0)
                                vrecip(eg, eg, [], [B_eg])
                                vtt(gsil[par], pbank[PB][:, 0:512], eg, ALU.mult, [B_eg], [B_gsil[par], Bp[PB]])
                        return sl, tail

                    def scan(idx, filler=None, ftail=None):
                        filler = filler if filler is not None else []
                        filler_tail = [ftail]
                        npts = [7 if job == "p" else 5]

                        def fill():
                            if filler:
                                n = -(-len(filler) // max(npts[0], 1))
                                for _ in range(n):
                                    if filler:
                                        filler.pop(0)()
                            npts[0] -= 1
                        d, seq, oi, ti, first, lastt = tiles[idx]
                        par = idx % 2
                        slot = (seq * 2 + ti) if job == "p" else ti
                        corder = [0, 1, 2, 3] if d == 0 else [3, 2, 1, 0]
                        if first:
                            if job == "p":
                                memset("gpsimd", S32, 0.0, [B_S32])
                            else:
                                P.dma("sync", S32[:, 0:256], sgla_d[l, d], writes=[B_S32])
                                for jj in range(2):
                                    P.dma("sync", S32[:, 256 + jj * 128:384 + jj * 128], shg_d[l, d, jj], writes=[B_S32])
                        if d == 1:
                            P.dma("sync", ofs[par], of_d[slot].rearrange("p a b -> p (a b)"), reads=[B_of[slot]], writes=[B_ofs[par]])
                        act(k_bf, k32[par], AF.Copy, [B_k32[par]], [B_kbf])
                        for h in range(3):
                            tr(ptb[:, h * 128:(h + 1) * 128], k_bf[:, h * 128:(h + 1) * 128], identb, [B_kbf, B_idb], [Bp[7]],
                               signal=(h == 2))
                        for h in range(3):
                            mm(pbank[PE_][:, h * 128:(h + 1) * 128], gg[par][:, h * 128:(h + 1) * 128], TRI[(h == 0, d)][0],
                               True, True, [B_gg[par], B_con], [Bp[PE_]], signal=(h == 2))
                        if job == "p":
                            fill()
                        act(ET, pbank[PE_][:, 0:384], AF.Exp, [], [B_ET, Bp[PE_]])
                        act(EinvT, pbank[PE_][:, 0:384], AF.Exp, [], [B_EinvT, Bp[PE_]], scale=-1.0)
                        mm(pbank[PE_][:, 0:128], TRI[(True, d)][1], gg[par][:, 0:128], True, True, [B_gg[par], B_con], [Bp[PE_]], signal=False)
                        mm(pbank[PE_][:, 128:384], TRI[(False, d)][1], gg[par][:, 128:384], True, True, [B_gg[par], B_con], [Bp[PE_]])
                        if job == "p":
                            fill()
                        act(er, pbank[PE_][:, 0:384], AF.Exp, [], [B_er, Bp[PE_]])
                        vtt(qeT, qs[par], ET, ALU.mult, [B_qs[par], B_ET], [B_qeT])
                        vtt(keT, ptb[:, 0:384], EinvT, ALU.mult, [B_EinvT], [B_keT, Bp[7]])
                        vtt(kd, k32[par], er, ALU.mult, [B_k32[par], B_er], [B_kd])
                        for c in range(4):
                            act(kdc[c], kd, AF.Identity, [B_kd, B_con], [B_kdc[c]], scale=cm[:, c:c + 1])
                        for h in range(3):
                            mm(pbank[PE_][:, h * 128:(h + 1) * 128], keT[:, h * 128:(h + 1) * 128], qeT[:, h * 128:(h + 1) * 128],
                               True, True, [B_keT, B_qeT], [Bp[PE_]], signal=(h == 2))
                        fill()
                        vtt(attm, pbank[PE_][:, 0:384], MASK3[d], ALU.mult, [B_con], [B_attm, Bp[PE_]])
                        chk("mx_d")
                        act(Sbf[0], S32, AF.Copy, [B_S32], [B_Sbf[0]])
                        for ci, c in enumerate(corder):
                            mm(pbank[PU][:, 0:256], kdc[c][:, 0:128], v_bf[par][:, 0:256], True, True,
                               [B_kdc[c], B_vbf[par]], [Bp[PU]], signal=False)
                            mm(pbank[PU][:, 256:384], kdc[c][:, 128:256], v_bf[par][:, 256:384], True, True,
                               [B_kdc[c], B_vbf[par]], [Bp[PU]], signal=False)
                            mm(pbank[PU][:, 384:512], kdc[c][:, 256:384], v_bf[par][:, 384:512], True, True,
                               [B_kdc[c], B_vbf[par]], [Bp[PU]])
                            fill()
                            dcol = (32 * c + 31) if d == 0 else 32 * c
                            for (a0, a1, h) in ((0, 256, 0), (256, 384, 1), (384, 512, 2)):
                                vstt(S32[:, a0:a1], S32[:, a0:a1], ET[:, h * 128 + dcol:h * 128 + dcol + 1], pbank[PU][:, a0:a1],
                                     ALU.mult, ALU.add, [B_ET], [B_S32, Bp[PU]])
                            if ci < 3:
                                act(Sbf[ci + 1], S32, AF.Copy, [B_S32], [B_Sbf[ci + 1]])
                        if filler_tail[0] is not None:
                            while filler:
                                filler.pop(0)()
                            filler_tail[0]()
                            filler_tail[0] = None
                        for u in range(4):
                            h = 0 if u < 2 else u - 1
                            mm(pbank[PO][:, u * 128:(u + 1) * 128], v_bf[par][:, u * 128:(u + 1) * 128], attm[:, h * 128:(h + 1) * 128],
                               True, False, [B_vbf[par], B_attm], [Bp[PO]], signal=False)
                            for ci, c in enumerate(corder):
                                mm(pbank[PO][:, u * 128 + 32 * c:u * 128 + 32 * c + 32], Sbf[ci][:, u * 128:(u + 1) * 128],
                                   qeT[:, h * 128 + 32 * c:h * 128 + 32 * c + 32], False, ci == 3,
                                   [B_Sbf[ci], B_qeT], [Bp[PO]], signal=(ci == 3 and u == 3))
                        chk("mx_e")
                        fill()
                        if d == 0:
                            act(oTs[par], pbank[PO][:, 0:512], AF.Copy, [], [B_oTs[par], Bp[PO]])
                            P.dma("sync", of_d[slot].rearrange("p a b -> p (a b)"), oTs[par], reads=[B_oTs[par]], writes=[B_of[slot]])
                        else:
                            vtt(otot, pbank[PO][:, 0:512], ofs[par], ALU.add, [B_ofs[par]], [B_otot, Bp[PO]])
                            act(sq4, otot, AF.Square, [B_otot], [B_sq4])
                            mm(pbank[PE_][:, 0:128], onesb, sq4[:, 0:128], True, False, [B_ones, B_sq4], [Bp[PE_]], signal=False)
                            mm(pbank[PE_][:, 0:128], onesb, sq4[:, 128:256], False, True, [B_ones, B_sq4], [Bp[PE_]], signal=False)
                            mm(pbank[PE_][:, 128:256], onesb, sq4[:, 256:384], True, True, [B_ones, B_sq4], [Bp[PE_]], signal=False)
                            mm(pbank[PE_][:, 256:384], onesb, sq4[:, 384:512], True, True, [B_ones, B_sq4], [Bp[PE_]])
                            fill()
                            vts(rstd4[:, 0:128], pbank[PE_][:, 0:128], 1.0 / 256, EPS, ALU.mult, ALU.add, [], [B_rstd4, Bp[PE_]])
                            vts(rstd4[:, 256:512], pbank[PE_][:, 128:384], 1.0 / 128, EPS, ALU.mult, ALU.add, [], [B_rstd4, Bp[PE_]])
                            act(rstd4[:, 0:128], rstd4[:, 0:128], AF.Ln, [], [B_rstd4])
                            act(rstd4[:, 256:512], rstd4[:, 256:512], AF.Ln, [], [B_rstd4])
                            act(rstd4[:, 128:256], rstd4[:, 0:128], AF.Exp, [], [B_rstd4], scale=-0.5)
                            act(rstd4[:, 0:128], rstd4[:, 0:128], AF.Exp, [], [B_rstd4], scale=-0.5)
                            act(rstd4[:, 256:512], rstd4[:, 256:512], AF.Exp, [], [B_rstd4], scale=-0.5)
                            vtt(mtmp, otot, rstd4, ALU.mult, [B_otot, B_rstd4], [B_mtmp])
                            gtt(mtmp, mtmp, gsil[par], ALU.mult, [B_gsil[par]], [B_mtmp])
                            for u in range(4):
                                gcol = u if u < 2 else 2
                                if job == "p":
                                    dst, Bd = PR["mTp"][:, gp * 4 + u, seq * 256 + ti * 128:seq * 256 + ti * 128 + 128], PR["B_mTp"]
                                else:
                                    dst, Bd = mTs[:, u, :], B_mTs
                                act(dst, mtmp[:, u * 128:(u + 1) * 128], AF.Identity, [B_mtmp, B_gn], [Bd], scale=gnT[:, l, gcol:gcol + 1])
                            if job == "s":
                                for u in range(4):
                                    tr(ptb[:, u * 128:(u + 1) * 128], mTs[:, u, :], identb, [B_mTs, B_idb], [Bp[7]],
                                       signal=(u == 3))
                                vcopy(mtok[par], ptb[:, 0:512], [], [B_mtok[par], Bp[7]])
                                if not col_major:
                                    P.dma("sync", mb_d[ti * 128:(ti + 1) * 128, :], mtok[par],
                                          reads=[B_mtok[par]], writes=[B_mb[ti]])
                                else:
                                    for half in range(2):
                                        P.dma("sync", mbv[2 * ti + half], mtok[par][half * 64:(half + 1) * 64, :],
                                              reads=[B_mtok[par]], writes=[B_mb[ti]])
                        TC[0] += 1
                        chk("mx_t%d" % TC[0])
                        if job == "p" and lastt:
                            P.dma("sync", ng_d[seq, l, d, gp], S32[:, 0:256], reads=[B_S32], writes=[B_out])
                            for jj in range(2):
                                P.dma("sync", nh_d[seq, l, d, 2 * gp + jj], S32[:, 256 + jj * 128:384 + jj * 128],
                                      reads=[B_S32], writes=[B_out])

                    if job == "s":
                        load_htok(tiles[0][3], 0)
                    def hoist(nxt):
                        if not PIPE:
                            return False
                        return job == "p" or tiles[nxt][0] in PIPE_S_DIRS
                    if job == "s":
                        load_htok(tiles[0][3], 0)
                    pending_tail = None
                    have = False
                    for idx in range(len(tiles)):
                        if not have:
                            sl, tl = proj_parts(idx)
                            for f in sl:
                                f()
                            tl()
                        have = False
                        if idx + 1 < len(tiles) and hoist(idx + 1):
                            sl, tl = proj_parts(idx + 1)
                            scan(idx, sl, tl)
                            have = True
                        else:
                            scan(idx)
                if job == "s":
                    for q in range(4):
                        allgather(mb_d[q * 1024:(q + 1) * 1024, :], mf_d[q].rearrange("r i f -> (r i) f"), B_mb, [B_mf])
                P.barrier()
                A.release()

            def p3(job, l):
                A.mark()
                last = (l == L - 1)
                mset = 0 if job == "p" else 1
                NB = 1 if job == "p" else 2
                NT = NB * 512
                blk0 = 0 if job == "p" else 1
                xT = A.alloc([KC, NT], F32)
                B_x = [[Buf() for _ in range(NB)] for _ in range(KC)]
                if job == "p":
                    aT, B_aT = PR["mTp"], PR["B_mTp"]
                    hT2, B_hT2 = PR["hTp"], PR["B_hTp"]
                else:
                    aT = A.alloc([KC, NT], BF16)
                    B_aT = Buf()
                    hT2, B_hT2 = aT, B_aT
                scr = [A.alloc([D], F32) for _ in range(2)]
                B_scr = [Buf(), Buf()]
                wt = [A.alloc([KC, 256], BF16) for _ in range(2)]
                B_wt = [Buf(), Buf()]
                w2t = [A.alloc([2, D], BF16) for _ in range(2)]
                B_w2t = [Buf(), Buf()]
                ub = [A.alloc([2, NT], BF16) for _ in range(2)]
                B_ub = [Buf(), Buf()]
                rl = [A.alloc([512], F32) for _ in range(2)]
                B_rl = [Buf(), Buf()]
                sqt = [A.alloc([512], BF16) for _ in range(2)]
                B_sq = [Buf(), Buf()]
                rstd = A.alloc([512], F32)
                B_rstd = Buf()
                tmp = [A.alloc([512], F32) for _ in range(2)]
                B_tmp = [Buf(), Buf()]
                for b in range(NB):
                    blk = blk0 + b
                    P.dma("sync", xT[:, :, b * 512:(b + 1) * 512], xTv[:, :, blk * 512:(blk + 1) * 512],
                          reads=[B_xT[blk]], writes=[B_x[kc][b] for kc in range(KC)])
                if job == "s":
                    selt = A.alloc([4], F32)
                    B_sel = Buf()
                    P.dma("sync", selt, sel_d, writes=[B_sel])
                    cand = [scr[i_ // 2].bitcast(BF16)[:, (i_ % 2) * 2048:(i_ % 2 + 1) * 2048].rearrange("p (a b) -> p a b", a=4)
                            for i_ in range(4)]
                    B_cand = [Buf() for _ in range(4)]
                    mtk = A.alloc([4, 512], BF16)
                    B_mtk = Buf()
                    for t in range(8):
                        for q in range(4):
                            P.dma("sync", cand[q], mf_d[q].rearrange("r i f -> i r f")[t * 128:(t + 1) * 128],
                                  reads=[B_mf], writes=[B_cand[q]])
                        vts(mtk, cand[0], selt[:, 0:1], None, ALU.mult, ALU.bypass, [B_cand[0], B_sel], [B_mtk])
                        for q in range(1, 4):
                            vstt(mtk, cand[q], selt[:, q:q + 1], mtk, ALU.mult, ALU.add, [B_cand[q], B_sel], [B_mtk])
                        for half in range(2):
                            for i in range(8):
                                kc = half * 8 + i
                                tr(ptb[:, i * 128:(i + 1) * 128], mtk[:, kc // 4, (kc % 4) * 128:(kc % 4 + 1) * 128], identb,
                                   [B_mtk, B_idb], [Bp[7]], signal=(i == 7))
                            vcopy(aT[:, half * 8:(half + 1) * 8, t * 128:(t + 1) * 128],
                                  ptb.rearrange("p (a b) -> p a b", a=8), [], [B_aT, Bp[7]])
                rr = [0]

                def nextbank():
                    rr[0] = (rr[0] + 1) % 6
                    return rr[0]

                wv = wout_d[l].rearrange("(k p) n -> p k n", p=128)
                for nb in range(8):
                    s = nb % 2
                    P.dma("gpsimd", wt[s], wv[:, :, nb * 256:(nb + 1) * 256], writes=[B_wt[s]])
                    for j in range(2):
                        n = nb * 2 + j
                        for b in range(NB):
                            pb = nextbank()
                            for kc in range(KC):
                                mm(pbank[pb][:, 0:512], wt[s][:, kc, j * 128:(j + 1) * 128], aT[:, kc, b * 512:(b + 1) * 512],
                                   kc == 0, kc == KC - 1, [B_wt[s], B_aT], [Bp[pb]], signal=(kc == KC - 1))
                            xs_ = xT[:, n, b * 512:(b + 1) * 512]
                            vstt(xs_, pbank[pb][:, 0:512], MV(l, mset, 2)[:, n:n + 1], xs_, ALU.mult, ALU.add,
                                 [B_mv], [B_x[n][b], Bp[pb]])
                for b in range(NB):
                    rms_rstd([xT[:, kc, b * 512:(b + 1) * 512] for kc in range(KC)], KC, 512,
                             [B_x[kc][b] for kc in range(KC)], sqt, B_sq, 0, rstd, B_rstd, 1.0 / D)
                    for kc in range(KC):
                        s = kc % 2
                        vtt(tmp[s], xT[:, kc, b * 512:(b + 1) * 512], rstd, ALU.mult, [B_x[kc][b], B_rstd], [B_tmp[s]])
                        act(hT2[:, kc, b * 512:(b + 1) * 512], tmp[s], AF.Identity, [B_tmp[s], B_mv], [B_hT2],
                            scale=MV(l, mset, 3)[:, kc:kc + 1], bias=MV(l, mset, 4)[:, kc:kc + 1])
                w1v = w1_d[l].rearrange("(k p) n -> p k n", p=128)
                w2v = w2_d[l].rearrange("(c p) n -> p c n", p=128)
                ri = 0
                for hg in range(32):
                    s = hg % 2
                    P.dma("gpsimd", wt[s], w1v[:, :, hg * 256:(hg + 1) * 256], writes=[B_wt[s]])
                    P.dma("gpsimd", w2t[s], w2v[:, hg * 2:(hg + 1) * 2, :], writes=[B_w2t[s]])
                    for j in range(2):
                        for b in range(NB):
                            pb = nextbank()
                            for kc in range(KC):
                                mm(pbank[pb][:, 0:512], wt[s][:, kc, j * 128:(j + 1) * 128], hT2[:, kc, b * 512:(b + 1) * 512],
                                   kc == 0, kc == KC - 1, [B_wt[s], B_hT2], [Bp[pb]], signal=(kc == KC - 1))
                            r_ = ri % 2
                            ri += 1
                            act(rl[r_], pbank[pb][:, 0:512], AF.Relu, [], [B_rl[r_], Bp[pb]])
                            gtt(ub[s][:, j, b * 512:(b + 1) * 512], rl[r_], rl[r_], ALU.mult, [B_rl[r_]], [B_ub[s]])
                    for n in range(KC):
                        for b in range(NB):
                            pb = nextbank()
                            for j in range(2):
                                mm(pbank[pb][:, 0:512], w2t[s][:, j, n * 128:(n + 1) * 128], ub[s][:, j, b * 512:(b + 1) * 512],
                                   j == 0, j == 1, [B_w2t[s], B_ub[s]], [Bp[pb]], signal=(j == 1))
                            xs_ = xT[:, n, b * 512:(b + 1) * 512]
                            vstt(xs_, pbank[pb][:, 0:512], MV(l, mset, 5)[:, n:n + 1], xs_, ALU.mult, ALU.add,
                                 [B_mv], [B_x[n][b], Bp[pb]])
                if not last:
                    for b in range(NB):
                        blk = blk0 + b
                        P.dma("sync", xTv[:, :, blk * 512:(blk + 1) * 512], xT[:, :, b * 512:(b + 1) * 512],
                              reads=[B_x[kc][b] for kc in range(KC)], writes=[B_xT[blk]])
                else:
                    ytok = scr
                    B_ytok = B_scr
                    if job == "s":
                        B_ytok = [B_cand[1], B_cand[3]]
                    ydst = yp_d if job == "p" else ys_d
                    for b in range(NB):
                        rms_rstd([xT[:, kc, b * 512:(b + 1) * 512] for kc in range(KC)], KC, 512,
                                 [B_x[kc][b] for kc in range(KC)], sqt, B_sq, 0, rstd, B_rstd, 1.0 / D)
                        for kc in range(KC):
                            xs_ = xT[:, kc, b * 512:(b + 1) * 512]
                            vstt(xs_, xs_, fgT[:, kc:kc + 1], rstd, ALU.mult, ALU.mult, [B_fg, B_rstd], [B_x[kc][b]])
                        for t in range(4):
                            s = t % 2
                            for q in range(4):
                                pb = nextbank()
                                for i in range(4):
                                    kc = q * 4 + i
                                    tr(pbank[pb][:, i * 128:(i + 1) * 128], xT[:, kc, b * 512 + t * 128:b * 512 + (t + 1) * 128],
                                       identf, [B_x[kc][b], B_con], [Bp[pb]], signal=(i == 3))
                                act(ytok[s][:, q * 512:(q + 1) * 512], pbank[pb][:, 0:512], AF.Copy, [], [B_ytok[s], Bp[pb]])
                            row0 = b * 512 + t * 128
                            P.dma("sync", ydst[row0:row0 + 128, :], ytok[s], reads=[B_ytok[s]], writes=[B_out])
                P.barrier()
                A.release()

            for l in range(L):
                A.mark()
                PR["hTp"] = A.alloc([KC, 512], BF16)
                PR["B_hTp"] = Buf()
                PR["mTp"] = A.alloc([KC, 512], BF16)
                PR["B_mTp"] = Buf()
                p1("p", l)
                chk("p1p%d" % l)
                mixer("p", l)
                chk("mxp%d" % l)
                p3("p", l)
                chk("p3p%d" % l)
                A.release()
                p1("s", l)
                chk("p1s%d" % l)
                mixer("s", l)
                chk("mxs%d" % l)
                p3("s", l)
                chk("p3s%d" % l)

        try:
            body()
        except _Stop:
            pass
        P.emit(nc, st)
    return nc


def _consts():
    s = np.arange(128)[:, None]
    t = np.arange(128)[None, :]
    same = (s // 32) == (t // 32)
    tri_f = (same & (s <= t)).astype(np.float32)
    tri_b = (same & (s >= t)).astype(np.float32)
    trir_f = (same & (s > t)).astype(np.float32)
    trir_b = (same & (s < t)).astype(np.float32)
    c = np.zeros((128, NCON, 128), np.float32)
    c[:, 0] = np.eye(128, dtype=np.float32)
    c[:, 1], c[:, 2], c[:, 3], c[:, 4] = tri_f, tri_b, trir_f, trir_b
    sc = np.float32(-1.0 / 16.0)
    c[:, 5], c[:, 6], c[:, 7], c[:, 8] = tri_f * sc, tri_b * sc, trir_f * sc, trir_b * sc
    for k in range(4):
        c[k * 32:(k + 1) * 32, 9, k] = 1.0
    for k in range(3):
        c[:, 10 + k] = tri_f
        c[:, 13 + k] = tri_b
    return c


def _group_cols(gp):
    r = lambda a, n: list(range(a, a + n))
    j0, j1 = 2 * gp, 2 * gp + 1
    cols = []
    cols += r(0 + gp * 128, 128) + r(3104 + j0 * 128, 128) + r(3104 + j1 * 128, 128)
    cols += r(2048 + gp * 256, 256) + r(7200 + j0 * 128, 128) + r(7200 + j1 * 128, 128)
    cols += r(3072, 16) + r(3088, 16)
    cols += r(1024 + gp * 256, 256) + r(6176 + j0 * 128, 128) + r(6176 + j1 * 128, 128)
    for d in range(2):
        cols += r(512 + gp * 128, 128) + r(4128 + d * 1024 + j0 * 128, 128) + r(4128 + d * 1024 + j1 * 128, 128)
    assert len(cols) == WG
    return np.array(cols)


_NC_CACHE = {}


def prepare_inputs(x_prompt, x_sample, state_gla, state_hgrn, c, c_ctx, ada_w, ada_b, norm1_g, norm2_g, w_in,
                   gla_gate_w, gla_gate_b, gla_norm_g, hgrn_lb, hgrn_norm_g, w_out, w_mlp1, w_mlp2, final_g,
                   cores=range(8)):
    f32 = lambda a: np.ascontiguousarray(np.asarray(a, dtype=np.float32))
    x_prompt, x_sample, state_gla, state_hgrn = f32(x_prompt), f32(x_sample), f32(state_gla), f32(state_hgrn)
    c, c_ctx, ada_w, ada_b = f32(c), f32(c_ctx), f32(ada_w), f32(ada_b)
    norm1_g, norm2_g, w_in, final_g = f32(norm1_g), f32(norm2_g), f32(w_in), f32(final_g)
    gla_gate_w, gla_gate_b, gla_norm_g = f32(gla_gate_w), f32(gla_gate_b), f32(gla_norm_g)
    hgrn_lb, hgrn_norm_g, w_out, w_mlp1, w_mlp2 = f32(hgrn_lb), f32(hgrn_norm_g), f32(w_out), f32(w_mlp1), f32(w_mlp2)

    gcols = [_group_cols(gp) for gp in range(4)]
    wing = np.ascontiguousarray(np.stack([np.stack([w_in[l][:, gcols[gp]] for gp in range(4)]) for l in range(L)]))
    gw = np.zeros((L, 4, 32, 2, 128), np.float32)
    for l in range(L):
        for gp in range(4):
            for d in range(2):
                gw[l, gp, 0:16, d, :] = gla_gate_w[l, d, :, gp * 128:(gp + 1) * 128]
                gw[l, gp, 16, d, :] = gla_gate_b[l, d, gp * 128:(gp + 1) * 128]
    lbg = np.ascontiguousarray(np.stack([hgrn_lb[:, :, gp * 256:(gp + 1) * 256] for gp in range(4)]))
    gn = np.zeros((L, 128, 3), np.float32)
    gn[:, :, 0] = gla_norm_g[:, 0:128]
    gn[:, :, 1] = gla_norm_g[:, 128:256]
    gn[:, :, 2] = hgrn_norm_g
    perm = np.zeros(D, np.int64)
    for gp in range(4):
        for vc in range(4):
            base = (gp * 256 + vc * 128) if vc < 2 else (1024 + (2 * gp + vc - 2) * 128)
            perm[gp * 512 + vc * 128:gp * 512 + (vc + 1) * 128] = base + np.arange(128)
    wout_p = np.ascontiguousarray(w_out[:, perm, :])
    tT = lambda v: np.ascontiguousarray(v.reshape(KC, 128).T)
    g1T = np.stack([tT(norm1_g[l]) for l in range(L)])
    g2T = np.stack([tT(norm2_g[l]) for l in range(L)])
    fgT = tT(final_g)
    ada_flat = np.concatenate([ada_w[l] for l in range(L)], axis=1)
    adab_flat = ada_b.reshape(-1)
    consts = _consts()

    in_maps = []
    for core in cores:
        g, r = core // 4, core % 4
        cv = np.stack([c_ctx, c[g]], axis=-1)
        m = {
            "xp": np.ascontiguousarray(x_prompt[2 * core:2 * core + 2].reshape(NPT, D)),
            "xs": np.ascontiguousarray(x_sample[g, r * NST:(r + 1) * NST]),
            "sgla": np.ascontiguousarray(state_gla[g, :, :, r]),
            "shg": np.ascontiguousarray(state_hgrn[g, :, :, 2 * r:2 * r + 2]),
            "cT": np.ascontiguousarray(cv.reshape(KC, 128, 2).transpose(1, 0, 2)),
            "adaw": np.ascontiguousarray(ada_flat[:, r * 6144:(r + 1) * 6144]),
            "adab": np.ascontiguousarray(adab_flat[r * 6144:(r + 1) * 6144].reshape(48, 128).T),
            "g1T": g1T, "g2T": g2T, "fgT": fgT,
            "wing": wing,
            "wins": np.ascontiguousarray(wing[:, r]),
            "gw": gw,
            "gws": np.ascontiguousarray(gw[:, r]),
            "lbg": lbg,
            "lbs": np.ascontiguousarray(lbg[r]),
            "gn": gn,
            "wout": wout_p, "w1": w_mlp1, "w2": w_mlp2,
            "consts": consts,
            "sel": np.ascontiguousarray(np.tile(np.eye(4, dtype=np.float32)[r][None, :], (128, 1))),
        }
        in_maps.append(m)
    return in_maps


def kernel(**inputs):
    if "nc" not in _NC_CACHE:
        _NC_CACHE["nc"] = build_program()
    nc = _NC_CACHE["nc"]
    in_maps = prepare_inputs(**inputs)
    res = run_bass_kernel_spmd(nc, in_maps, core_ids=list(range(8)))
    outs = res.results
    y_prompt = np.concatenate([outs[cc]["yp"].reshape(2, 256, D) for cc in range(8)], axis=0).astype(np.float32)
    y_sample = np.stack([np.concatenate([outs[g * 4 + r]["ys"] for r in range(4)], axis=0) for g in range(2)]).astype(np.float32)
    new_gla = np.concatenate([outs[cc]["ng"] for cc in range(8)], axis=0).astype(np.float32)
    new_hgrn = np.concatenate([outs[cc]["nh"] for cc in range(8)], axis=0).astype(np.float32)
    return (y_prompt, y_sample, new_gla, new_hgrn)
```

```python
import contextlib
import numpy as np
import concourse.bass as bass
import concourse.mybir as mybir
from concourse.bass_utils import run_bass_kernel_spmd

F32 = mybir.dt.float32
BF16 = mybir.dt.bfloat16
AF = mybir.ActivationFunctionType
ALU = mybir.AluOpType

ENGS = ["tensor", "vector", "scalar", "gpsimd", "sync"]
NDSEM = 8

L = 2
D = 2048
KC = 16
NPT = 512
NST = 1024
TS = 4096
WG = 2208
C_Q = (0, 128, 256)
C_GATE = 384
C_GLR = (896, 912)
C_V = 928
C_K = (1440, 1824)
EPS = 1e-6
NCON = 16
PIPE = True
PIPE_S_DIRS = (0, 1)


class Buf:
    __slots__ = ("lw", "rd")

    def __init__(self):
        self.lw = None
        self.rd = []


class Prog:
    def __init__(self):
        self.q = {e: [] for e in ENGS}
        self.cnt = {}
        self.seen = {e: {} for e in ENGS}
        self.dma_rr = {e: 0 for e in ENGS}
        self.open = {e: False for e in ENGS}

    @staticmethod
    def _deps(reads, writes):
        deps = []
        for b in reads:
            if b.lw is not None:
                deps.append(b.lw)
        for b in writes:
            if b.lw is not None:
                deps.append(b.lw)
            deps.extend(b.rd)
        return deps

    def _waits(self, eng, deps):
        best = {}
        for k, v in deps:
            if v > best.get(k, 0):
                best[k] = v
        out = []
        for k, v in best.items():
            if eng == "tensor" and k == "e_tensor":
                continue
            if self.seen[eng].get(k, 0) < v:
                self.seen[eng][k] = v
                out.append((k, v))
        return out

    @staticmethod
    def _mark(reads, writes, key, v):
        for b in reads:
            b.rd.append((key, v))
        for b in writes:
            b.lw = (key, v)
            b.rd = []

    def op(self, eng, fn, reads=(), writes=(), signal=True):
        waits = self._waits(eng, self._deps(reads, writes))
        key = "e_" + eng
        v = self.cnt.get(key, 0) + 1
        if signal:
            self.cnt[key] = v
            self.q[eng].append((waits, fn, key, 1))
            self.open[eng] = False
        else:
            self.q[eng].append((waits, fn, None, 0))
            self.open[eng] = True
        self._mark(reads, writes, key, v)

    def dma(self, eng, out, in_, reads=(), writes=(), **kw):
        slot = self.dma_rr[eng] % NDSEM
        self.dma_rr[eng] += 1
        key = "d_%s_%d" % (eng, slot)
        prev = self.cnt.get(key, 0)
        deps = self._deps(reads, writes)
        if prev:
            deps.append((key, prev))
        waits = self._waits(eng, deps)
        v = prev + 16
        self.cnt[key] = v
        self.q[eng].append((waits, lambda e: e.dma_start(out=out, in_=in_, **kw), key, 16))
        self._mark(reads, writes, key, v)

    def barrier(self):
        for e in ENGS:
            assert not self.open[e], "open unsignalled group on %s" % e
        tot = list(self.cnt.items())
        for e in ENGS:
            waits = self._waits(e, tot)
            if waits:
                self.q[e].append((waits, None, None, 0))

    def emit(self, nc, stack):
        self.barrier()
        sems = {k: stack.enter_context(nc.semaphore(k)) for k in self.cnt}
        block = stack.enter_context(nc.Block())
        prog = self

        def run(engname, e):
            for waits, fn, key, amt in prog.q[engname]:
                for k, v in waits:
                    e.wait_ge(sems[k], v)
                if fn is None:
                    continue
                ins = fn(e)
                if key is not None:
                    ins.then_inc(sems[key], amt)

        @block.tensor
        def _(e):
            run("tensor", e)

        @block.vector
        def _(e):
            run("vector", e)

        @block.scalar
        def _(e):
            run("scalar", e)

        @block.gpsimd
        def _(e):
            run("gpsimd", e)

        @block.sync
        def _(e):
            run("sync", e)


class Arena:
    def __init__(self, t, nwords):
        self.t = t
        self.n = nwords
        self.top = 0
        self.stack = []

    def alloc(self, shape, dtype, parts=128):
        nel = int(np.prod(shape))
        nw = (nel * (2 if dtype == BF16 else 4) + 3) // 4
        nw = (nw + 7) // 8 * 8
        off = self.top
        self.top += nw
        assert self.top <= self.n, "arena overflow %d > %d" % (self.top, self.n)
        v = self.t[0:parts, off:off + nw]
        if dtype == BF16:
            v = v.bitcast(BF16)
        v = v[:, 0:nel]
        if len(shape) == 2:
            v = v.rearrange("p (a b) -> p a b", a=shape[0])
        elif len(shape) == 3:
            v = v.rearrange("p (a b c) -> p a b c", a=shape[0], b=shape[1])
        return v

    def mark(self):
        self.stack.append(self.top)

    def release(self):
        self.top = self.stack.pop()


class _Stop(Exception):
    pass


def build_program(stop=None, fake_cc=False, ncores=8, lite=()):
    nc = bass.Bass("TRN2", target_bir_lowering=False)
    P = Prog()

    def chk(name):
        if stop == name:
            raise _Stop()

    def din(name, shape, dt=F32):
        if name in lite:
            shape = [1] * len(shape)
        return nc.dram_tensor(name, list(shape), dt, kind="ExternalInput").ap()

    def dout(name, shape):
        return nc.dram_tensor(name, list(shape), F32, kind="ExternalOutput").ap()

    def dint(name, shape, dt):
        return nc.dram_tensor(name, list(shape), dt, kind="Internal").ap()

    xp_d = din("xp", [NPT, D])
    xs_d = din("xs", [NST, D])
    sgla_d = din("sgla", [L, 2, 128, 256])
    shg_d = din("shg", [L, 2, 2, 128, 128])
    cT_d = din("cT", [128, KC, 2])
    adaw_d = din("adaw", [D, 6144])
    adab_d = din("adab", [128, 48])
    g1_d = din("g1T", [L, 128, KC])
    g2_d = din("g2T", [L, 128, KC])
    fg_d = din("fgT", [128, KC])
    wing_d = din("wing", [L, 4, D, WG])
    wins_d = din("wins", [L, D, WG])
    gw_d = din("gw", [L, 4, 32, 2, 128])
    gws_d = din("gws", [L, 32, 2, 128])
    lbg_d = din("lbg", [4, L, 2, 256])
    lbs_d = din("lbs", [L, 2, 256])
    gn_d = din("gn", [L, 128, 3])
    wout_d = din("wout", [L, D, D])
    w1_d = din("w1", [L, D, 4 * D])
    w2_d = din("w2", [L, 4 * D, D])
    con_d = din("consts", [128, NCON, 128])
    sel_d = din("sel", [128, 4])

    yp_d = dout("yp", [NPT, D])
    ys_d = dout("ys", [NST, D])
    ng_d = dout("ng", [2, L, 2, 4, 128, 256])
    nh_d = dout("nh", [2, L, 2, 8, 128, 128])

    xT_d = dint("xT_scr", [KC, 128, NPT + NST], F32)
    hb_d = dint("h_bounce", [NST, D], BF16)
    hf_d = dint("h_full", [4, 4, 256, D], BF16)
    mb_d = dint("m_bounce", [TS, 512], BF16)
    mf_d = dint("m_full", [4, 4, 1024, 512], BF16)
    modb_d = dint("mod_bounce", [128, 96], F32)
    modf_d = dint("mod_full", [512, 96], F32)
    of_d = dint("of_scr", [32, 128, 4, 128], F32)
    RG = [[0, 1, 2, 3], [4, 5, 6, 7]] if ncores == 8 else [[0, 1, 2, 3]]

    B_xT = [Buf() for _ in range(3)]
    B_hb = Buf()
    B_hf = Buf()
    B_mb = [Buf() for _ in range(32)]
    B_mf = Buf()
    B_of = [Buf() for _ in range(32)]
    B_out = Buf()

    with contextlib.ExitStack() as st:
        AW = 52000
        arena_t = st.enter_context(nc.sbuf_tensor("arena", [128, AW], F32))
        A = Arena(arena_t, AW)
        pbank = [st.enter_context(nc.psum_tensor("pb%d" % i, [128, 512], F32)) for i in range(8)]
        Bp = [Buf() for _ in range(8)]
        ptb = pbank[7][:, :].bitcast(BF16)

        def body():
            def act(out, in_, func, R, W, **kw):
                P.op("scalar", lambda e: e.activation(out=out, in_=in_, func=func, **kw), R, W)

            def vtt(out, a, b, op, R, W):
                P.op("vector", lambda e: e.tensor_tensor(out=out, in0=a, in1=b, op=op), R, W)

            def gtt(out, a, b, op, R, W):
                P.op("gpsimd", lambda e: e.tensor_tensor(out=out, in0=a, in1=b, op=op), R, W)

            def vts(out, a, s1, s2, op0, op1, R, W):
                P.op("vector", lambda e: e.tensor_scalar(out=out, in0=a, scalar1=s1, scalar2=s2, op0=op0, op1=op1), R, W)

            def gts(out, a, s1, s2, op0, op1, R, W):
                P.op("gpsimd", lambda e: e.tensor_scalar(out=out, in0=a, scalar1=s1, scalar2=s2, op0=op0, op1=op1), R, W)

            def vstt(out, a, s, b, op0, op1, R, W):
                P.op("vector", lambda e: e.scalar_tensor_tensor(out=out, in0=a, scalar=s, in1=b, op0=op0, op1=op1), R, W)

            def vcopy(out, in_, R, W):
                P.op("vector", lambda e: e.tensor_copy(out=out, in_=in_), R, W)

            def gcopy(out, in_, R, W):
                P.op("gpsimd", lambda e: e.tensor_copy(out=out, in_=in_), R, W)

            def vrecip(out, in_, R, W):
                P.op("vector", lambda e: e.reciprocal(out=out, in_=in_), R, W)

            def mm(out, lhsT, rhs, start, stop, R, W, signal=True):
                P.op("tensor", lambda e: e.matmul(out=out, lhsT=lhsT, rhs=rhs, start=start, stop=stop), R, W, signal)

            def tr(out, in_, ident, R, W, signal=True):
                P.op("tensor", lambda e: e.transpose(out=out, in_=in_, identity=ident), R, W, signal)

            def memset(eng, ap, val, W):
                P.op(eng, lambda e: e.memset(ap, val), (), W)

            def allgather(src, dst, R, W):
                if fake_cc:
                    P.dma("sync", dst[0:src.shape[0]], src, reads=R, writes=W)
                else:
                    P.op("gpsimd", lambda e: e.collective_compute("AllGather", ALU.bypass, replica_groups=RG,
                                                                  ins=[src], outs=[dst]), R, W)

            con = A.alloc([NCON, 128], F32)
            B_con = Buf()
            identf = con[:, 0, :]
            TRI = {
                (False, 0): (con[:, 1, :], con[:, 3, :]), (False, 1): (con[:, 2, :], con[:, 4, :]),
                (True, 0): (con[:, 5, :], con[:, 7, :]), (True, 1): (con[:, 6, :], con[:, 8, :]),
            }
            MASK3 = (con[:, 10:13, :].rearrange("p a b -> p (a b)"), con[:, 13:16, :].rearrange("p a b -> p (a b)"))
            cm = con[:, 9, 0:4]
            identb = A.alloc([128], BF16)
            B_idb = Buf()
            onesb = A.alloc([128], BF16)
            B_ones = Buf()
            mv = A.alloc([L * 2 * 6, KC], F32)
            B_mv = Buf()
            fgT = A.alloc([KC], F32)
            B_fg = Buf()
            gnT = A.alloc([L, 3], F32)
            B_gn = Buf()

            def MV(l, s, w):
                return mv[:, (l * 2 + s) * 6 + w, :]

            P.dma("sync", con, con_d, writes=[B_con])
            P.dma("gpsimd", identb, con_d[:, 0, :], writes=[B_idb])
            memset("vector", onesb, 1.0, [B_ones])
            P.dma("sync", fgT, fg_d, writes=[B_fg])
            P.dma("sync", gnT, gn_d.rearrange("l p c -> p l c"), writes=[B_gn])

            A.mark()
            cT = A.alloc([KC, 2], F32)
            B_cT = Buf()
            sc_e = A.alloc([KC, 2], F32)
            scT = A.alloc([KC, 2], F32)
            B_sc = Buf()
            adb = A.alloc([48], F32)
            B_adb = Buf()
            g12 = A.alloc([2 * L, KC], F32)
            B_g12 = Buf()
            awt = [A.alloc([KC, 512], F32) for _ in range(2)]
            B_aw = [Buf() for _ in range(2)]
            modown = A.alloc([48, 2], F32)
            B_mo = Buf()
            modall = A.alloc([4, 96], F32)
            B_ma = Buf()
            P.dma("sync", cT, cT_d, writes=[B_cT])
            P.dma("sync", adb, adab_d, writes=[B_adb])
            for l in range(L):
                P.dma("sync", g12[:, l, :], g1_d[l], writes=[B_g12])
                P.dma("sync", g12[:, L + l, :], g2_d[l], writes=[B_g12])
            act(sc_e, cT, AF.Exp, [B_cT], [B_sc], scale=-1.0)
            vts(sc_e, sc_e, 1.0, None, ALU.add, ALU.bypass, [], [B_sc])
            vrecip(sc_e, sc_e, [], [B_sc])
            vtt(scT, sc_e, cT, ALU.mult, [B_cT], [B_sc])
            adaw_v = adaw_d.rearrange("(k p) n -> p k n", p=128)
            for t in range(12):
                s = t % 2
                P.dma("sync", awt[s], adaw_v[:, :, t * 512:(t + 1) * 512], writes=[B_aw[s]])
                for j in range(4):
                    n = t * 4 + j
                    for kc in range(KC):
                        mm(pbank[0][:, 2 * n:2 * n + 2], awt[s][:, kc, j * 128:(j + 1) * 128], scT[:, kc, :],
                           kc == 0, kc == KC - 1, [B_aw[s], B_sc], [Bp[0]], signal=(kc == KC - 1))
            pm = pbank[0][:, 0:96].rearrange("p (n v) -> p n v", v=2)
            for v in range(2):
                vtt(modown[:, :, v], pm[:, :, v], adb, ALU.add, [B_adb], [B_mo, Bp[0]])
            P.dma("sync", modb_d, modown.rearrange("p n v -> p (n v)"), reads=[B_mo], writes=[B_hb])
            B_modf = Buf()
            allgather(modb_d, modf_d, [B_hb], [B_modf])
            P.dma("sync", modall, modf_d.rearrange("(r p) f -> p r f", p=128), reads=[B_modf], writes=[B_ma])
            modn = modall.rearrange("p r (j v) -> p (r j) v", v=2)
            for l in range(L):
                for s in range(2):
                    def sl(w):
                        return modn[:, l * 96 + w * 16:l * 96 + (w + 1) * 16, s]
                    vstt(MV(l, s, 0), sl(1), 1.0, g12[:, l, :], ALU.add, ALU.mult, [B_ma, B_g12], [B_mv])
                    vcopy(MV(l, s, 1), sl(0), [B_ma], [B_mv])
                    vcopy(MV(l, s, 2), sl(2), [B_ma], [B_mv])
                    vstt(MV(l, s, 3), sl(4), 1.0, g12[:, L + l, :], ALU.add, ALU.mult, [B_ma, B_g12], [B_mv])
                    vcopy(MV(l, s, 4), sl(3), [B_ma], [B_mv])
                    vcopy(MV(l, s, 5), sl(5), [B_ma], [B_mv])
            P.barrier()
            A.release()
            chk("p0")

            A.mark()
            xtok = [A.alloc([D], F32) for _ in range(2)]
            B_xtok = [Buf() for _ in range(2)]
            xTb = [A.alloc([KC, 512], F32) for _ in range(2)]
            B_xTb = [Buf() for _ in range(2)]
            xTv = xT_d.rearrange("k p t -> p k t")
            for blk in range(3):
                src = xp_d if blk == 0 else xs_d
                bs = blk % 2
                for t in range(4):
                    ts_ = t % 2
                    row0 = (t if blk == 0 else (blk - 1) * 4 + t) * 128
                    P.dma("sync", xtok[ts_], src[row0:row0 + 128, :], writes=[B_xtok[ts_]])
                    for q in range(4):
                        pb = q % 2
                        for i in range(4):
                            kc = q * 4 + i
                            tr(pbank[pb][:, i * 128:(i + 1) * 128], xtok[ts_][:, kc * 128:(kc + 1) * 128], identf,
                               [B_xtok[ts_], B_con], [Bp[pb]], signal=(i == 3))
                        o_ = xTb[bs][:, q * 4:(q + 1) * 4, t * 128:(t + 1) * 128]
                        i_ = pbank[pb][:, :].rearrange("p (a b) -> p a b", a=4)
                        if q % 2:
                            act(o_, i_, AF.Copy, [], [B_xTb[bs], Bp[pb]])
                        else:
                            vcopy(o_, i_, [], [B_xTb[bs], Bp[pb]])
                P.dma("sync", xTv[:, :, blk * 512:(blk + 1) * 512], xTb[bs], reads=[B_xTb[bs]], writes=[B_xT[blk]])
            P.barrier()
            A.release()
            chk("p0x")

            def rms_rstd(srcs, n, ncols, B_src, sqt, B_sq, pb, rstd, B_rstd, inv_n):
                for k in range(n):
                    s = k % 2
                    act(sqt[s], srcs[k], AF.Square, B_src, [B_sq[s]])
                    mm(pbank[pb][:, 0:ncols], onesb, sqt[s], k == 0, k == n - 1, [B_ones, B_sq[s]], [Bp[pb]])
                vts(rstd, pbank[pb][:, 0:ncols], inv_n, EPS, ALU.mult, ALU.add, [], [B_rstd, Bp[pb]])
                act(rstd, rstd, AF.Ln, [], [B_rstd])
                act(rstd, rstd, AF.Exp, [], [B_rstd], scale=-0.5)

            PR = {}

            def p1(job, l):
                A.mark()
                mset = 0 if job == "p" else 1
                blks = [0] if job == "p" else [1, 2]
                xt = A.alloc([KC, 512], F32)
                B_xt = Buf()
                sqt = [A.alloc([512], BF16) for _ in range(2)]
                B_sq = [Buf(), Buf()]
                rstd = A.alloc([512], F32)
                B_rstd = Buf()
                tmp = [A.alloc([512], F32) for _ in range(2)]
                B_tmp = [Buf(), Buf()]
                if job == "p":
                    hT, B_hT = PR["hTp"], PR["B_hTp"]
                else:
                    hT = A.alloc([KC, 512], BF16)
                    B_hT = Buf()
                    htok = [A.alloc([D], BF16) for _ in range(2)]
                    B_htok = [Buf(), Buf()]
                for blk in blks:
                    P.dma("sync", xt, xTv[:, :, blk * 512:(blk + 1) * 512], reads=[B_xT[blk]], writes=[B_xt])
                    rms_rstd([xt[:, kc, :] for kc in range(KC)], KC, 512, [B_xt], sqt, B_sq, 0, rstd, B_rstd, 1.0 / D)
                    for kc in range(KC):
                        s = kc % 2
                        vtt(tmp[s], xt[:, kc, :], rstd, ALU.mult, [B_xt, B_rstd], [B_tmp[s]])
                        act(hT[:, kc, :], tmp[s], AF.Identity, [B_tmp[s], B_mv], [B_hT],
                            scale=MV(l, mset, 0)[:, kc:kc + 1], bias=MV(l, mset, 1)[:, kc:kc + 1])
                    if job == "s":
                        for t in range(4):
                            s = t % 2
                            for half in range(2):
                                for i in range(8):
                                    kc = half * 8 + i
                                    tr(ptb[:, i * 128:(i + 1) * 128], hT[:, kc, t * 128:(t + 1) * 128], identb,
                                       [B_hT, B_idb], [Bp[7]], signal=(i == 7))
                                vcopy(htok[s][:, half * 1024:(half + 1) * 1024], ptb, [], [B_htok[s], Bp[7]])
                            tt = (blk - 1) * 4 + t
                            if l % 2 == 0:
                                P.dma("sync", hb_d[tt * 128:(tt + 1) * 128, :], htok[s], reads=[B_htok[s]], writes=[B_hb])
                            else:
                                hbv = hb_d.rearrange("(c rl) d -> rl c d", rl=16)
                                for half in range(2):
                                    P.dma("sync", hbv[2 * tt + half], htok[s][half * 64:(half + 1) * 64, :],
                                          reads=[B_htok[s]], writes=[B_hb])
                if job == "s":
                    for j in range(4):
                        allgather(hb_d[j * 256:(j + 1) * 256, :], hf_d[j].rearrange("r i d -> (r i) d"), [B_hb], [B_hf])
                P.barrier()
                A.release()

            TC = [0]

            def mixer(job, l):
                A.mark()
                npass = 4 if job == "p" else 1
                col_major = (job == "s") and (l % 2 == 1)
                wg = A.alloc([KC, WG], BF16)
                B_wg = Buf()
                gwt = A.alloc([2, 128], BF16, parts=32)
                B_gwt = Buf()
                lbraw = A.alloc([L * 2 * 256], F32)
                B_lbraw = Buf()
                lbv = A.alloc([2, 256], F32)
                oml = A.alloc([2, 256], F32)
                B_lb = Buf()

                def T2(shape, dt):
                    return [A.alloc(shape, dt) for _ in range(2)], [Buf(), Buf()]

                gaug = [A.alloc([128], BF16, parts=32) for _ in range(2)]
                B_gaug = [Buf(), Buf()]
                qs, B_qs = T2([384], F32)
                k32, B_k32 = T2([384], F32)
                hfr, B_hfr = T2([256], F32)
                e0g, B_e0g = T2([128], F32)
                v_bf, B_vbf = T2([512], BF16)
                gsil, B_gsil = T2([512], F32)
                eg = A.alloc([512], F32)
                B_eg = Buf()
                fh = A.alloc([256], F32)
                B_fh = Buf()
                gg, B_gg = T2([384], F32)
                eq = A.alloc([256], F32)
                B_eq = Buf()
                k_bf = A.alloc([384], BF16)
                B_kbf = Buf()
                ET = A.alloc([384], F32)
                B_ET = Buf()
                EinvT = A.alloc([384], F32)
                B_EinvT = Buf()
                er = A.alloc([384], F32)
                B_er = Buf()
                qeT = A.alloc([384], BF16)
                B_qeT = Buf()
                keT = A.alloc([384], BF16)
                B_keT = Buf()
                kd = A.alloc([384], F32)
                B_kd = Buf()
                kdc = [A.alloc([384], BF16) for _ in range(4)]
                B_kdc = [Buf() for _ in range(4)]
                attm = A.alloc([384], BF16)
                B_attm = Buf()
                S32 = A.alloc([512], F32)
                B_S32 = Buf()
                Sbf = [A.alloc([512], BF16) for _ in range(4)]
                B_Sbf = [Buf() for _ in range(4)]
                otot = A.alloc([512], F32)
                B_otot = Buf()
                ofs, B_ofs = T2([512], F32)
                oTs, B_oTs = T2([512], F32)
                sq4 = A.alloc([512], BF16)
                B_sq4 = Buf()
                rstd4 = A.alloc([512], F32)
                B_rstd4 = Buf()
                mtmp = A.alloc([512], F32)
                B_mtmp = Buf()
                if job == "s":
                    htok, B_htok = T2([D], BF16)
                    hTs, B_hTs = T2([KC, 128], BF16)
                    mTs = A.alloc([4, 128], BF16)
                    B_mTs = Buf()
                    mtok, B_mtok = T2([512], BF16)
                for g_ in gaug:
                    memset("gpsimd", g_, 1.0, [B_gaug[0], B_gaug[1]])
                mbv = mb_d.rearrange("(row c) f -> c row f", c=64)
                PA, PB, PC, PD, PE_, PO, PU = 0, 1, 2, 3, 4, 5, 6

                def load_htok(ti, s):
                    if not col_major:
                        P.dma("sync", htok[s], hf_d[(ti % 8) // 2, ti // 8, ((ti % 8) % 2) * 128:((ti % 8) % 2) * 128 + 128, :],
                              reads=[B_hf], writes=[B_htok[s]])
                    else:
                        for half in range(2):
                            c = 2 * ti + half
                            for r in range(4):
                                P.dma("sync", htok[s][half * 64 + r * 16:half * 64 + (r + 1) * 16, :],
                                      hf_d[c // 16, r, (c % 16) * 16:(c % 16) * 16 + 16, :], reads=[B_hf], writes=[B_htok[s]])

                for ps_ in range(npass):
                    gp = ps_
                    wsrc = (wing_d[l, gp] if job == "p" else wins_d[l]).rearrange("(k p) n -> p k n", p=128)
                    for hh in range(2):
                        P.dma("gpsimd", wg[:, :, hh * 1104:(hh + 1) * 1104], wsrc[:, :, hh * 1104:(hh + 1) * 1104],
                              writes=[B_wg])
                    P.dma("gpsimd", gwt, (gw_d[l, gp] if job == "p" else gws_d[l]), writes=[B_gwt])
                    lsrc = (lbg_d[gp] if job == "p" else lbs_d).rearrange("l d n -> (l d n)").partition_broadcast(128)
                    P.dma("sync", lbraw, lsrc, writes=[B_lbraw])
                    lbv2 = lbv.rearrange("p d n -> p (d n)")
                    oml2 = oml.rearrange("p d n -> p (d n)")
                    if l == 0:
                        memset("vector", lbv2, 0.0, [B_lb])
                        memset("vector", oml2, 1.0, [B_lb])
                    else:
                        vtt(lbv2, lbraw[:, 512:1024], lbraw[:, 0:512], ALU.subtract, [B_lbraw], [B_lb])
                        act(lbv2, lbv2, AF.Exp, [], [B_lb], scale=-1.0)
                        vts(lbv2, lbv2, 1.0, None, ALU.add, ALU.bypass, [], [B_lb])
                        vrecip(lbv2, lbv2, [], [B_lb])
                        vts(oml2, lbv2, -1.0, 1.0, ALU.mult, ALU.add, [], [B_lb])
                    chk("mx_a")
                    nseq, nt = (2, 2) if job == "p" else (1, 32)

                    tiles = []
                    for d in range(2):
                        for seq in range(nseq):
                            order = list(range(nt)) if d == 0 else list(range(nt - 1, -1, -1))
                            for oi, ti in enumerate(order):
                                tiles.append((d, seq, oi, ti, oi == 0, oi == nt - 1))

                    def get_hT(idx):
                        d, seq, oi, ti, first, lastt = tiles[idx]
                        par = idx % 2
                        if job == "p":
                            c0 = seq * 256 + ti * 128
                            return [PR["hTp"][:, kc, c0:c0 + 128] for kc in range(KC)], PR["B_hTp"]
                        for half in range(2):
                            for i in range(8):
                                kc = half * 8 + i
                                tr(ptb[:, i * 128:(i + 1) * 128], htok[par][:, kc * 128:(kc + 1) * 128], identb,
                                   [B_htok[par], B_idb], [Bp[7]], signal=(i == 7))
                            vcopy(hTs[par][:, half * 8:(half + 1) * 8, :],
                                  ptb.rearrange("p (a b) -> p a b", a=8), [], [B_hTs[par], Bp[7]])
                        return [hTs[par][:, kc, :] for kc in range(KC)], B_hTs[par]

                    def proj_parts(idx):
                        d, seq, oi, ti, first, lastt = tiles[idx]
                        par = idx % 2
                        sl = []
                        if job == "p":
                            c0 = seq * 256 + ti * 128
                            hTk = [PR["hTp"][:, kc, c0:c0 + 128] for kc in range(KC)]
                            B_hTk = PR["B_hTp"]
                        else:
                            hTk = [hTs[par][:, kc, :] for kc in range(KC)]
                            B_hTk = B_hTs[par]

                            def p_h(half):
                                def f():
                                    if half == 0 and idx + 1 < len(tiles):
                                        load_htok(tiles[idx + 1][3], (idx + 1) % 2)
                                    for i in range(8):
                                        kc = half * 8 + i
                                        tr(ptb[:, i * 128:(i + 1) * 128], htok[par][:, kc * 128:(kc + 1) * 128], identb,
                                           [B_htok[par], B_idb], [Bp[7]], signal=(i == 7))
                                    vcopy(hTs[par][:, half * 8:(half + 1) * 8, :],
                                          ptb.rearrange("p (a b) -> p a b", a=8), [], [B_hTs[par], Bp[7]])
                                return f
                            sl.append(p_h(0))
                            sl.append(p_h(1))

                        def grp(out, wcol0, wn, fm):
                            def f():
                                for kc in range(KC):
                                    if fm:
                                        mm(out, wg[:, kc, wcol0:wcol0 + wn], hTk[kc], kc == 0, kc == KC - 1,
                                           [B_wg, B_hTk], [Bp[out_bank[id(out)]]], signal=(kc == KC - 1))
                                    else:
                                        mm(out, hTk[kc], wg[:, kc, wcol0:wcol0 + wn], kc == 0, kc == KC - 1,
                                           [B_wg, B_hTk], [Bp[out_bank[id(out)]]], signal=(kc == KC - 1))
                            return f
                        out_bank = {}

                        def reg(ap, bank):
                            out_bank[id(ap)] = bank
                            return ap
                        sl.append(grp(reg(pbank[PA][0:16, 384:512], PA), C_GLR[d], 16, True))
                        sl.append(grp(reg(pbank[PD][:, 0:384], PD), C_K[d], 384, False))

                        def p_glogit():
                            act(gaug[par][0:16, :], pbank[PA][0:16, 384:512], AF.Copy, [], [B_gaug[par], Bp[PA]])
                            mm(pbank[PD][:, 384:512], gaug[par], gwt[:, d, :], True, True, [B_gaug[par], B_gwt], [Bp[PD]])
                        sl.append(p_glogit)
                        sl.append(grp(reg(pbank[PC][:, 0:512], PC), C_V, 512, False))
                        for qi in range(3):
                            sl.append(grp(reg(pbank[PA][:, qi * 128:(qi + 1) * 128], PA), C_Q[qi], 128, True))
                        if d == 1:
                            for gi in range(4):
                                sl.append(grp(reg(pbank[PB][:, gi * 128:(gi + 1) * 128], PB), C_GATE + gi * 128, 128, True))

                        def tail():
                            vcopy(hfr[par], pbank[PD][:, 128:384], [], [B_hfr[par], Bp[PD]])
                            vcopy(k32[par][:, 0:128], pbank[PD][:, 0:128], [], [B_k32[par], Bp[PD]])
                            act(e0g[par], pbank[PD][:, 384:512], AF.Exp, [], [B_e0g[par], Bp[PD]], scale=-1.0)
                            act(fh, hfr[par], AF.Exp, [B_hfr[par]], [B_fh], scale=-1.0)
                            act(fh, fh, AF.Ln, [], [B_fh], bias=1.0)
                            act(fh, fh, AF.Exp, [], [B_fh], scale=-1.0)
                            gtt(fh, fh, oml[:, d, :], ALU.mult, [B_lb], [B_fh])
                            gtt(fh, fh, lbv[:, d, :], ALU.add, [B_lb], [B_fh])
                            act(gg[par][:, 128:384], fh, AF.Ln, [B_fh], [B_gg[par]])
                            act(k32[par][:, 128:384], fh, AF.Identity, [B_fh], [B_k32[par]], scale=-1.0, bias=1.0)
                            act(gg[par][:, 0:128], e0g[par], AF.Ln, [B_e0g[par]], [B_gg[par]], bias=1.0)
                            act(v_bf[par], pbank[PC][:, 0:512], AF.Copy, [], [B_vbf[par], Bp[PC]])
                            act(qs[par][:, 0:128], pbank[PA][:, 0:128], AF.Copy, [], [B_qs[par], Bp[PA]], scale=128.0 ** -0.5)
                            vcopy(qs[par][:, 128:384], pbank[PA][:, 128:384], [], [B_qs[par], Bp[PA]])
                            act(eq, qs[par][:, 128:384], AF.Exp, [B_qs[par]], [B_eq], scale=-1.0)
                            act(eq, eq, AF.Ln, [], [B_eq], bias=1.0)
                            act(eq, eq, AF.Exp, [], [B_eq], scale=-1.0)
                            vtt(qs[par][:, 128:384], qs[par][:, 128:384], eq, ALU.mult, [B_eq], [B_qs[par]])
                            if d == 1:
                                act(eg, pbank[PB][:, 0:512], AF.Exp, [], [B_eg, Bp[PB]], scale=-1.0)
                                act(eg, eg, AF.Ln, [], [B_eg], bias=1.0)
                                act(eg, eg, AF.Exp, [], [B_eg], scale=-1.0)
                                vtt(gsil[par], pbank[PB][:, 0:512], eg, ALU.mult, [B_eg], [B_gsil[par], Bp[PB]])
                        return sl, tail

                    def scan(idx, filler=None):
                        filler = filler if filler is not None else []
                        npts = [(8 if job == "p" else 6) + (1 if tiles[idx][0] == 1 else 0)]

                        def fill():
                            if filler:
                                n = -(-len(filler) // max(npts[0], 1))
                                for _ in range(n):
                                    if filler:
                                        filler.pop(0)()
                            npts[0] -= 1
                        d, seq, oi, ti, first, lastt = tiles[idx]
                        par = idx % 2
                        slot = (seq * 2 + ti) if job == "p" else ti
                        corder = [0, 1, 2, 3] if d == 0 else [3, 2, 1, 0]
                        if first:
                            if job == "p":
                                memset("gpsimd", S32, 0.0, [B_S32])
                            else:
                                P.dma("sync", S32[:, 0:256], sgla_d[l, d], writes=[B_S32])
                                for jj in range(2):
                                    P.dma("sync", S32[:, 256 + jj * 128:384 + jj * 128], shg_d[l, d, jj], writes=[B_S32])
                        if d == 1:
                            P.dma("sync", ofs[par], of_d[slot].rearrange("p a b -> p (a b)"), reads=[B_of[slot]], writes=[B_ofs[par]])
                        act(k_bf, k32[par], AF.Copy, [B_k32[par]], [B_kbf])
                        for h in range(3):
                            tr(ptb[:, h * 128:(h + 1) * 128], k_bf[:, h * 128:(h + 1) * 128], identb, [B_kbf, B_idb], [Bp[7]],
                               signal=(h == 2))
                        for h in range(3):
                            mm(pbank[PE_][:, h * 128:(h + 1) * 128], gg[par][:, h * 128:(h + 1) * 128], TRI[(h == 0, d)][0],
                               True, True, [B_gg[par], B_con], [Bp[PE_]], signal=(h == 2))
                        if job == "p":
                            fill()
                        act(ET, pbank[PE_][:, 0:384], AF.Exp, [], [B_ET, Bp[PE_]])
                        act(EinvT, pbank[PE_][:, 0:384], AF.Exp, [], [B_EinvT, Bp[PE_]], scale=-1.0)
                        mm(pbank[PE_][:, 0:128], TRI[(True, d)][1], gg[par][:, 0:128], True, True, [B_gg[par], B_con], [Bp[PE_]], signal=False)
                        mm(pbank[PE_][:, 128:384], TRI[(False, d)][1], gg[par][:, 128:384], True, True, [B_gg[par], B_con], [Bp[PE_]])
                        if job == "p":
                            fill()
                        act(er, pbank[PE_][:, 0:384], AF.Exp, [], [B_er, Bp[PE_]])
                        vtt(qeT, qs[par], ET, ALU.mult, [B_qs[par], B_ET], [B_qeT])
                        vtt(keT, ptb[:, 0:384], EinvT, ALU.mult, [B_EinvT], [B_keT, Bp[7]])
                        vtt(kd, k32[par], er, ALU.mult, [B_k32[par], B_er], [B_kd])
                        for c in range(4):
                            act(kdc[c], kd, AF.Identity, [B_kd, B_con], [B_kdc[c]], scale=cm[:, c:c + 1])
                        for h in range(3):
                            mm(pbank[PE_][:, h * 128:(h + 1) * 128], keT[:, h * 128:(h + 1) * 128], qeT[:, h * 128:(h + 1) * 128],
                               True, True, [B_keT, B_qeT], [Bp[PE_]], signal=(h == 2))
                        fill()
                        vtt(attm, pbank[PE_][:, 0:384], MASK3[d], ALU.mult, [B_con], [B_attm, Bp[PE_]])
                        chk("mx_d")
                        act(Sbf[0], S32, AF.Copy, [B_S32], [B_Sbf[0]])
                        for ci, c in enumerate(corder):
                            mm(pbank[PU][:, 0:256], kdc[c][:, 0:128], v_bf[par][:, 0:256], True, True,
                               [B_kdc[c], B_vbf[par]], [Bp[PU]], signal=False)
                            mm(pbank[PU][:, 256:384], kdc[c][:, 128:256], v_bf[par][:, 256:384], True, True,
                               [B_kdc[c], B_vbf[par]], [Bp[PU]], signal=False)
                            mm(pbank[PU][:, 384:512], kdc[c][:, 256:384], v_bf[par][:, 384:512], True, True,
                               [B_kdc[c], B_vbf[par]], [Bp[PU]])
                            fill()
                            dcol = (32 * c + 31) if d == 0 else 32 * c
                            for (a0, a1, h) in ((0, 256, 0), (256, 384, 1), (384, 512, 2)):
                                vstt(S32[:, a0:a1], S32[:, a0:a1], ET[:, h * 128 + dcol:h * 128 + dcol + 1], pbank[PU][:, a0:a1],
                                     ALU.mult, ALU.add, [B_ET], [B_S32, Bp[PU]])
                            if ci < 3:
                                act(Sbf[ci + 1], S32, AF.Copy, [B_S32], [B_Sbf[ci + 1]])
                        for u in range(4):
                            h = 0 if u < 2 else u - 1
                            mm(pbank[PO][:, u * 128:(u + 1) * 128], v_bf[par][:, u * 128:(u + 1) * 128], attm[:, h * 128:(h + 1) * 128],
                               True, False, [B_vbf[par], B_attm], [Bp[PO]], signal=False)
                            for ci, c in enumerate(corder):
                                mm(pbank[PO][:, u * 128 + 32 * c:u * 128 + 32 * c + 32], Sbf[ci][:, u * 128:(u + 1) * 128],
                                   qeT[:, h * 128 + 32 * c:h * 128 + 32 * c + 32], False, ci == 3,
                                   [B_Sbf[ci], B_qeT], [Bp[PO]], signal=(ci == 3 and u == 3))
                        chk("mx_e")
                        fill()
                        if d == 0:
                            act(oTs[par], pbank[PO][:, 0:512], AF.Copy, [], [B_oTs[par], Bp[PO]])
                            P.dma("sync", of_d[slot].rearrange("p a b -> p (a b)"), oTs[par], reads=[B_oTs[par]], writes=[B_of[slot]])
                        else:
                            vtt(otot, pbank[PO][:, 0:512], ofs[par], ALU.add, [B_ofs[par]], [B_otot, Bp[PO]])
                            act(sq4, otot, AF.Square, [B_otot], [B_sq4])
                            mm(pbank[PE_][:, 0:128], onesb, sq4[:, 0:128], True, False, [B_ones, B_sq4], [Bp[PE_]], signal=False)
                            mm(pbank[PE_][:, 0:128], onesb, sq4[:, 128:256], False, True, [B_ones, B_sq4], [Bp[PE_]], signal=False)
                            mm(pbank[PE_][:, 128:256], onesb, sq4[:, 256:384], True, True, [B_ones, B_sq4], [Bp[PE_]], signal=False)
                            mm(pbank[PE_][:, 256:384], onesb, sq4[:, 384:512], True, True, [B_ones, B_sq4], [Bp[PE_]])
                            fill()
                            vts(rstd4[:, 0:128], pbank[PE_][:, 0:128], 1.0 / 256, EPS, ALU.mult, ALU.add, [], [B_rstd4, Bp[PE_]])
                            vts(rstd4[:, 256:512], pbank[PE_][:, 128:384], 1.0 / 128, EPS, ALU.mult, ALU.add, [], [B_rstd4, Bp[PE_]])
                            act(rstd4[:, 0:128], rstd4[:, 0:128], AF.Ln, [], [B_rstd4])
                            act(rstd4[:, 256:512], rstd4[:, 256:512], AF.Ln, [], [B_rstd4])
                            act(rstd4[:, 128:256], rstd4[:, 0:128], AF.Exp, [], [B_rstd4], scale=-0.5)
                            act(rstd4[:, 0:128], rstd4[:, 0:128], AF.Exp, [], [B_rstd4], scale=-0.5)
                            act(rstd4[:, 256:512], rstd4[:, 256:512], AF.Exp, [], [B_rstd4], scale=-0.5)
                            vtt(mtmp, otot, rstd4, ALU.mult, [B_otot, B_rstd4], [B_mtmp])
                            gtt(mtmp, mtmp, gsil[par], ALU.mult, [B_gsil[par]], [B_mtmp])
                            for u in range(4):
                                gcol = u if u < 2 else 2
                                if job == "p":
                                    dst, Bd = PR["mTp"][:, gp * 4 + u, seq * 256 + ti * 128:seq * 256 + ti * 128 + 128], PR["B_mTp"]
                                else:
                                    dst, Bd = mTs[:, u, :], B_mTs
                                act(dst, mtmp[:, u * 128:(u + 1) * 128], AF.Identity, [B_mtmp, B_gn], [Bd], scale=gnT[:, l, gcol:gcol + 1])
                            if job == "s":
                                for u in range(4):
                                    tr(ptb[:, u * 128:(u + 1) * 128], mTs[:, u, :], identb, [B_mTs, B_idb], [Bp[7]],
                                       signal=(u == 3))
                                vcopy(mtok[par], ptb[:, 0:512], [], [B_mtok[par], Bp[7]])
                                if not col_major:
                                    P.dma("sync", mb_d[ti * 128:(ti + 1) * 128, :], mtok[par],
                                          reads=[B_mtok[par]], writes=[B_mb[ti]])
                                else:
                                    for half in range(2):
                                        P.dma("sync", mbv[2 * ti + half], mtok[par][half * 64:(half + 1) * 64, :],
                                              reads=[B_mtok[par]], writes=[B_mb[ti]])
                        TC[0] += 1
                        chk("mx_t%d" % TC[0])
                        if job == "p" and lastt:
                            P.dma("sync", ng_d[seq, l, d, gp], S32[:, 0:256], reads=[B_S32], writes=[B_out])
                            for jj in range(2):
                                P.dma("sync", nh_d[seq, l, d, 2 * gp + jj], S32[:, 256 + jj * 128:384 + jj * 128],
                                      reads=[B_S32], writes=[B_out])

                    if job == "s":
                        load_htok(tiles[0][3], 0)
                    def hoist(nxt):
                        if not PIPE:
                            return False
                        return job == "p" or tiles[nxt][0] in PIPE_S_DIRS
                    if job == "s":
                        load_htok(tiles[0][3], 0)
                    pending_tail = None
                    have = False
                    for idx in range(len(tiles)):
                        if not have:
                            sl, tl = proj_parts(idx)
                            for f in sl:
                                f()
                            tl()
                        have = False
                        if idx + 1 < len(tiles) and hoist(idx + 1):
                            sl, tl = proj_parts(idx + 1)
                            scan(idx, sl)
                            for f in sl:
                                f()
                            tl()
                            have = True
                        else:
                            scan(idx)
                if job == "s":
                    for q in range(4):
                        allgather(mb_d[q * 1024:(q + 1) * 1024, :], mf_d[q].rearrange("r i f -> (r i) f"), B_mb, [B_mf])
                P.barrier()
                A.release()

            def p3(job, l):
                A.mark()
                last = (l == L - 1)
                mset = 0 if job == "p" else 1
                NB = 1 if job == "p" else 2
                NT = NB * 512
                blk0 = 0 if job == "p" else 1
                xT = A.alloc([KC, NT], F32)
                B_x = [[Buf() for _ in range(NB)] for _ in range(KC)]
                if job == "p":
                    aT, B_aT = PR["mTp"], PR["B_mTp"]
                    hT2, B_hT2 = PR["hTp"], PR["B_hTp"]
                else:
                    aT = A.alloc([KC, NT], BF16)
                    B_aT = Buf()
                    hT2, B_hT2 = aT, B_aT
                scr = [A.alloc([D], F32) for _ in range(2)]
                B_scr = [Buf(), Buf()]
                wt = [A.alloc([KC, 256], BF16) for _ in range(2)]
                B_wt = [Buf(), Buf()]
                w2t = [A.alloc([2, D], BF16) for _ in range(2)]
                B_w2t = [Buf(), Buf()]
                ub = [A.alloc([2, NT], BF16) for _ in range(2)]
                B_ub = [Buf(), Buf()]
                rl = [A.alloc([512], F32) for _ in range(2)]
                B_rl = [Buf(), Buf()]
                sqt = [A.alloc([512], BF16) for _ in range(2)]
                B_sq = [Buf(), Buf()]
                rstd = A.alloc([512], F32)
                B_rstd = Buf()
                tmp = [A.alloc([512], F32) for _ in range(2)]
                B_tmp = [Buf(), Buf()]
                for b in range(NB):
                    blk = blk0 + b
                    P.dma("sync", xT[:, :, b * 512:(b + 1) * 512], xTv[:, :, blk * 512:(blk + 1) * 512],
                          reads=[B_xT[blk]], writes=[B_x[kc][b] for kc in range(KC)])
                if job == "s":
                    selt = A.alloc([4], F32)
                    B_sel = Buf()
                    P.dma("sync", selt, sel_d, writes=[B_sel])
                    cand = [scr[i_ // 2].bitcast(BF16)[:, (i_ % 2) * 2048:(i_ % 2 + 1) * 2048].rearrange("p (a b) -> p a b", a=4)
                            for i_ in range(4)]
                    B_cand = [Buf() for _ in range(4)]
                    mtk = A.alloc([4, 512], BF16)
                    B_mtk = Buf()
                    for t in range(8):
                        for q in range(4):
                            P.dma("sync", cand[q], mf_d[q].rearrange("r i f -> i r f")[t * 128:(t + 1) * 128],
                                  reads=[B_mf], writes=[B_cand[q]])
                        vts(mtk, cand[0], selt[:, 0:1], None, ALU.mult, ALU.bypass, [B_cand[0], B_sel], [B_mtk])
                        for q in range(1, 4):
                            vstt(mtk, cand[q], selt[:, q:q + 1], mtk, ALU.mult, ALU.add, [B_cand[q], B_sel], [B_mtk])
                        for half in range(2):
                            for i in range(8):
                                kc = half * 8 + i
                                tr(ptb[:, i * 128:(i + 1) * 128], mtk[:, kc // 4, (kc % 4) * 128:(kc % 4 + 1) * 128], identb,
                                   [B_mtk, B_idb], [Bp[7]], signal=(i == 7))
                            vcopy(aT[:, half * 8:(half + 1) * 8, t * 128:(t + 1) * 128],
                                  ptb.rearrange("p (a b) -> p a b", a=8), [], [B_aT, Bp[7]])
                rr = [0]

                def nextbank():
                    rr[0] = (rr[0] + 1) % 6
                    return rr[0]

                wv = wout_d[l].rearrange("(k p) n -> p k n", p=128)
                for nb in range(8):
                    s = nb % 2
                    P.dma("gpsimd", wt[s], wv[:, :, nb * 256:(nb + 1) * 256], writes=[B_wt[s]])
                    for j in range(2):
                        n = nb * 2 + j
                        for b in range(NB):
                            pb = nextbank()
                            for kc in range(KC):
                                mm(pbank[pb][:, 0:512], wt[s][:, kc, j * 128:(j + 1) * 128], aT[:, kc, b * 512:(b + 1) * 512],
                                   kc == 0, kc == KC - 1, [B_wt[s], B_aT], [Bp[pb]], signal=(kc == KC - 1))
                            xs_ = xT[:, n, b * 512:(b + 1) * 512]
                            vstt(xs_, pbank[pb][:, 0:512], MV(l, mset, 2)[:, n:n + 1], xs_, ALU.mult, ALU.add,
                                 [B_mv], [B_x[n][b], Bp[pb]])
                for b in range(NB):
                    rms_rstd([xT[:, kc, b * 512:(b + 1) * 512] for kc in range(KC)], KC, 512,
                             [B_x[kc][b] for kc in range(KC)], sqt, B_sq, 0, rstd, B_rstd, 1.0 / D)
                    for kc in range(KC):
                        s = kc % 2
                        vtt(tmp[s], xT[:, kc, b * 512:(b + 1) * 512], rstd, ALU.mult, [B_x[kc][b], B_rstd], [B_tmp[s]])
                        act(hT2[:, kc, b * 512:(b + 1) * 512], tmp[s], AF.Identity, [B_tmp[s], B_mv], [B_hT2],
                            scale=MV(l, mset, 3)[:, kc:kc + 1], bias=MV(l, mset, 4)[:, kc:kc + 1])
                w1v = w1_d[l].rearrange("(k p) n -> p k n", p=128)
                w2v = w2_d[l].rearrange("(c p) n -> p c n", p=128)
                ri = 0
                for hg in range(32):
                    s = hg % 2
                    P.dma("gpsimd", wt[s], w1v[:, :, hg * 256:(hg + 1) * 256], writes=[B_wt[s]])
                    P.dma("gpsimd", w2t[s], w2v[:, hg * 2:(hg + 1) * 2, :], writes=[B_w2t[s]])
                    for j in range(2):
                        for b in range(NB):
                            pb = nextbank()
                            for kc in range(KC):
                                mm(pbank[pb][:, 0:512], wt[s][:, kc, j * 128:(j + 1) * 128], hT2[:, kc, b * 512:(b + 1) * 512],
                                   kc == 0, kc == KC - 1, [B_wt[s], B_hT2], [Bp[pb]], signal=(kc == KC - 1))
                            r_ = ri % 2
                            ri += 1
                            act(rl[r_], pbank[pb][:, 0:512], AF.Relu, [], [B_rl[r_], Bp[pb]])
                            gtt(ub[s][:, j, b * 512:(b + 1) * 512], rl[r_], rl[r_], ALU.mult, [B_rl[r_]], [B_ub[s]])
                    for n in range(KC):
                        for b in range(NB):
                            pb = nextbank()
                            for j in range(2):
                                mm(pbank[pb][:, 0:512], w2t[s][:, j, n * 128:(n + 1) * 128], ub[s][:, j, b * 512:(b + 1) * 512],
                                   j == 0, j == 1, [B_w2t[s], B_ub[s]], [Bp[pb]], signal=(j == 1))
                            xs_ = xT[:, n, b * 512:(b + 1) * 512]
                            vstt(xs_, pbank[pb][:, 0:512], MV(l, mset, 5)[:, n:n + 1], xs_, ALU.mult, ALU.add,
                                 [B_mv], [B_x[n][b], Bp[pb]])
                if not last:
                    for b in range(NB):
                        blk = blk0 + b
                        P.dma("sync", xTv[:, :, blk * 512:(blk + 1) * 512], xT[:, :, b * 512:(b + 1) * 512],
                              reads=[B_x[kc][b] for kc in range(KC)], writes=[B_xT[blk]])
                else:
                    ytok = scr
                    B_ytok = B_scr
                    if job == "s":
                        B_ytok = [B_cand[1], B_cand[3]]
                    ydst = yp_d if job == "p" else ys_d
                    for b in range(NB):
                        rms_rstd([xT[:, kc, b * 512:(b + 1) * 512] for kc in range(KC)], KC, 512,
                                 [B_x[kc][b] for kc in range(KC)], sqt, B_sq, 0, rstd, B_rstd, 1.0 / D)
                        for kc in range(KC):
                            xs_ = xT[:, kc, b * 512:(b + 1) * 512]
                            vstt(xs_, xs_, fgT[:, kc:kc + 1], rstd, ALU.mult, ALU.mult, [B_fg, B_rstd], [B_x[kc][b]])
                        for t in range(4):
                            s = t % 2
                            for q in range(4):
                                pb = nextbank()
                                for i in range(4):
                                    kc = q * 4 + i
                                    tr(pbank[pb][:, i * 128:(i + 1) * 128], xT[:, kc, b * 512 + t * 128:b * 512 + (t + 1) * 128],
                                       identf, [B_x[kc][b], B_con], [Bp[pb]], signal=(i == 3))
                                act(ytok[s][:, q * 512:(q + 1) * 512], pbank[pb][:, 0:512], AF.Copy, [], [B_ytok[s], Bp[pb]])
                            row0 = b * 512 + t * 128
                            P.dma("sync", ydst[row0:row0 + 128, :], ytok[s], reads=[B_ytok[s]], writes=[B_out])
                P.barrier()
                A.release()

            for l in range(L):
                A.mark()
                PR["hTp"] = A.alloc([KC, 512], BF16)
                PR["B_hTp"] = Buf()
                PR["mTp"] = A.alloc([KC, 512], BF16)
                PR["B_mTp"] = Buf()
                p1("p", l)
                chk("p1p%d" % l)
                mixer("p", l)
                chk("mxp%d" % l)
                p3("p", l)
                chk("p3p%d" % l)
                A.release()
                p1("s", l)
                chk("p1s%d" % l)
                mixer("s", l)
                chk("mxs%d" % l)
                p3("s", l)
                chk("p3s%d" % l)

        try:
            body()
        except _Stop:
            pass
        P.emit(nc, st)
    return nc


def _consts():
    s = np.arange(128)[:, None]
    t = np.arange(128)[None, :]
    same = (s // 32) == (t // 32)
    tri_f = (same & (s <= t)).astype(np.float32)
    tri_b = (same & (s >= t)).astype(np.float32)
    trir_f = (same & (s > t)).astype(np.float32)
    trir_b = (same & (s < t)).astype(np.float32)
    c = np.zeros((128, NCON, 128), np.float32)
    c[:, 0] = np.eye(128, dtype=np.float32)
    c[:, 1], c[:, 2], c[:, 3], c[:, 4] = tri_f, tri_b, trir_f, trir_b
    sc = np.float32(-1.0 / 16.0)
    c[:, 5], c[:, 6], c[:, 7], c[:, 8] = tri_f * sc, tri_b * sc, trir_f * sc, trir_b * sc
    for k in range(4):
        c[k * 32:(k + 1) * 32, 9, k] = 1.0
    for k in range(3):
        c[:, 10 + k] = tri_f
        c[:, 13 + k] = tri_b
    return c


def _group_cols(gp):
    r = lambda a, n: list(range(a, a + n))
    j0, j1 = 2 * gp, 2 * gp + 1
    cols = []
    cols += r(0 + gp * 128, 128) + r(3104 + j0 * 128, 128) + r(3104 + j1 * 128, 128)
    cols += r(2048 + gp * 256, 256) + r(7200 + j0 * 128, 128) + r(7200 + j1 * 128, 128)
    cols += r(3072, 16) + r(3088, 16)
    cols += r(1024 + gp * 256, 256) + r(6176 + j0 * 128, 128) + r(6176 + j1 * 128, 128)
    for d in range(2):
        cols += r(512 + gp * 128, 128) + r(4128 + d * 1024 + j0 * 128, 128) + r(4128 + d * 1024 + j1 * 128, 128)
    assert len(cols) == WG
    return np.array(cols)


_NC_CACHE = {}


def prepare_inputs(x_prompt, x_sample, state_gla, state_hgrn, c, c_ctx, ada_w, ada_b, norm1_g, norm2_g, w_in,
                   gla_gate_w, gla_gate_b, gla_norm_g, hgrn_lb, hgrn_norm_g, w_out, w_mlp1, w_mlp2, final_g,
                   cores=range(8)):
    f32 = lambda a: np.ascontiguousarray(np.asarray(a, dtype=np.float32))
    x_prompt, x_sample, state_gla, state_hgrn = f32(x_prompt), f32(x_sample), f32(state_gla), f32(state_hgrn)
    c, c_ctx, ada_w, ada_b = f32(c), f32(c_ctx), f32(ada_w), f32(ada_b)
    norm1_g, norm2_g, w_in, final_g = f32(norm1_g), f32(norm2_g), f32(w_in), f32(final_g)
    gla_gate_w, gla_gate_b, gla_norm_g = f32(gla_gate_w), f32(gla_gate_b), f32(gla_norm_g)
    hgrn_lb, hgrn_norm_g, w_out, w_mlp1, w_mlp2 = f32(hgrn_lb), f32(hgrn_norm_g), f32(w_out), f32(w_mlp1), f32(w_mlp2)

    gcols = [_group_cols(gp) for gp in range(4)]
    wing = np.ascontiguousarray(np.stack([np.stack([w_in[l][:, gcols[gp]] for gp in range(4)]) for l in range(L)]))
    gw = np.zeros((L, 4, 32, 2, 128), np.float32)
    for l in range(L):
        for gp in range(4):
            for d in range(2):
                gw[l, gp, 0:16, d, :] = gla_gate_w[l, d, :, gp * 128:(gp + 1) * 128]
                gw[l, gp, 16, d, :] = gla_gate_b[l, d, gp * 128:(gp + 1) * 128]
    lbg = np.ascontiguousarray(np.stack([hgrn_lb[:, :, gp * 256:(gp + 1) * 256] for gp in range(4)]))
    gn = np.zeros((L, 128, 3), np.float32)
    gn[:, :, 0] = gla_norm_g[:, 0:128]
    gn[:, :, 1] = gla_norm_g[:, 128:256]
    gn[:, :, 2] = hgrn_norm_g
    perm = np.zeros(D, np.int64)
    for gp in range(4):
        for vc in range(4):
            base = (gp * 256 + vc * 128) if vc < 2 else (1024 + (2 * gp + vc - 2) * 128)
            perm[gp * 512 + vc * 128:gp * 512 + (vc + 1) * 128] = base + np.arange(128)
    wout_p = np.ascontiguousarray(w_out[:, perm, :])
    tT = lambda v: np.ascontiguousarray(v.reshape(KC, 128).T)
    g1T = np.stack([tT(norm1_g[l]) for l in range(L)])
    g2T = np.stack([tT(norm2_g[l]) for l in range(L)])
    fgT = tT(final_g)
    ada_flat = np.concatenate([ada_w[l] for l in range(L)], axis=1)
    adab_flat = ada_b.reshape(-1)
    consts = _consts()

    in_maps = []
    for core in cores:
        g, r = core // 4, core % 4
        cv = np.stack([c_ctx, c[g]], axis=-1)
        m = {
            "xp": np.ascontiguousarray(x_prompt[2 * core:2 * core + 2].reshape(NPT, D)),
            "xs": np.ascontiguousarray(x_sample[g, r * NST:(r + 1) * NST]),
            "sgla": np.ascontiguousarray(state_gla[g, :, :, r]),
            "shg": np.ascontiguousarray(state_hgrn[g, :, :, 2 * r:2 * r + 2]),
            "cT": np.ascontiguousarray(cv.reshape(KC, 128, 2).transpose(1, 0, 2)),
            "adaw": np.ascontiguousarray(ada_flat[:, r * 6144:(r + 1) * 6144]),
            "adab": np.ascontiguousarray(adab_flat[r * 6144:(r + 1) * 6144].reshape(48, 128).T),
            "g1T": g1T, "g2T": g2T, "fgT": fgT,
            "wing": wing,
            "wins": np.ascontiguousarray(wing[:, r]),
            "gw": gw,
            "gws": np.ascontiguousarray(gw[:, r]),
            "lbg": lbg,
            "lbs": np.ascontiguousarray(lbg[r]),
            "gn": gn,
            "wout": wout_p, "w1": w_mlp1, "w2": w_mlp2,
            "consts": consts,
            "sel": np.ascontiguousarray(np.tile(np.eye(4, dtype=np.float32)[r][None, :], (128, 1))),
        }
        in_maps.append(m)
    return in_maps


def kernel(**inputs):
    if "nc" not in _NC_CACHE:
        _NC_CACHE["nc"] = build_program()
    nc = _NC_CACHE["nc"]
    in_maps = prepare_inputs(**inputs)
    res = run_bass_kernel_spmd(nc, in_maps, core_ids=list(range(8)))
    outs = res.results
    y_prompt = np.concatenate([outs[cc]["yp"].reshape(2, 256, D) for cc in range(8)], axis=0).astype(np.float32)
    y_sample = np.stack([np.concatenate([outs[g * 4 + r]["ys"] for r in range(4)], axis=0) for g in range(2)]).astype(np.float32)
    new_gla = np.concatenate([outs[cc]["ng"] for cc in range(8)], axis=0).astype(np.float32)
    new_hgrn = np.concatenate([outs[cc]["nh"] for cc in range(8)], axis=0).astype(np.float32)
    return (y_prompt, y_sample, new_gla, new_hgrn)
```
